# Optimizing a Trainium2 kernel written in Bass

```python
import jax
import jax.numpy as jnp
from jax import lax
import numpy as np

D_MODEL = 1024
BATCH = 8
SEQ = 4096
DEPTH = 1

N_META = 16
CHUNK = 128
PAD = CHUNK - N_META
RET_HEADS = 4
RET_QK_DIM = 128
RET_V_DIM = 256
RET_QK = RET_HEADS * RET_QK_DIM
RET_V = RET_HEADS * RET_V_DIM
SSD_D_INNER = 2 * D_MODEL
SSD_HEAD_DIM = 64
SSD_HEADS = SSD_D_INNER // SSD_HEAD_DIM
SSD_GROUPS = 4
SSD_HPG = SSD_HEADS // SSD_GROUPS
SSD_STATE = 128
SSD_CONV = 3
SSD_XBC = SSD_D_INNER + 2 * SSD_GROUPS * SSD_STATE
D_FF = 2816
FFN_CONV = 3
EPS = 1e-6
ROPE_BASE = 10000.0
IN_SIZES = (RET_QK, RET_QK, RET_V, RET_V, SSD_D_INNER, SSD_XBC, SSD_HEADS, SSD_HEADS, D_MODEL, D_MODEL)
D_IN = sum(IN_SIZES)

kernel_name = 'hybrid_retnet_ssd_encoder_block'


def _split(t, sizes):
    out = []
    start = 0
    for s in sizes:
        out.append(t[..., start:start + s])
        start += s
    return out


def rms_norm(x, w):
    x32 = x.astype(jnp.float32)
    y = x32 * lax.rsqrt(jnp.mean(x32 * x32, axis=-1, keepdims=True) + EPS)
    return y.astype(x.dtype) * w


def group_rms_norm(x, w, groups):
    shp = x.shape
    xg = x.reshape(shp[:-1] + (groups, shp[-1] // groups)).astype(jnp.float32)
    y = xg * lax.rsqrt(jnp.mean(xg * xg, axis=-1, keepdims=True) + EPS)
    return y.reshape(shp).astype(x.dtype) * w


def head_group_norm(y):
    y32 = y.astype(jnp.float32)
    mu = jnp.mean(y32, axis=-1, keepdims=True)
    var = jnp.mean(jnp.square(y32 - mu), axis=-1, keepdims=True)
    return ((y32 - mu) * lax.rsqrt(var + EPS)).astype(y.dtype)


def dw_conv_centred(x, w, b):
    k = w.shape[0]
    y = lax.conv_general_dilated(x, w[:, None, :], window_strides=(1,), padding=[(k // 2, k // 2)],
                                 dimension_numbers=('NWC', 'WIO', 'NWC'), feature_group_count=x.shape[-1])
    return y + b


def rotary(x, pos):
    half = x.shape[-1] // 2
    inv = ROPE_BASE ** (-jnp.arange(half, dtype=jnp.float32) / half)
    ang = pos.astype(jnp.float32)[:, None] * inv[None, :]
    cos = jnp.cos(ang)[None, :, None, :].astype(x.dtype)
    sin = jnp.sin(ang)[None, :, None, :].astype(x.dtype)
    x1, x2 = x[..., :half], x[..., half:]
    return jnp.concatenate([x1 * cos - x2 * sin, x1 * sin + x2 * cos], axis=-1)


def pad_front(t):
    return jnp.pad(t, ((0, 0), (PAD, 0)) + ((0, 0),) * (t.ndim - 2))


def to_chunks(t):
    return t.reshape((t.shape[0], t.shape[1] // CHUNK, CHUNK) + t.shape[2:])


def from_chunks(t):
    return t.reshape((t.shape[0], t.shape[1] * CHUNK) + t.shape[3:])


def flip_seq(t):
    return jnp.flip(t, axis=1)


def exclusive_chunk_scan(states, decays):
    s = jnp.moveaxis(states, 1, 0)
    d = jnp.moveaxis(decays, 1, 0)

    def step(carry, sd):
        s_n, d_n = sd
        return carry * d_n + s_n, carry

    _, prev = lax.scan(step, jnp.zeros_like(s[0]), (s, d))
    return jnp.moveaxis(prev, 0, 1)


def retention_intra(qc, kc, vc, log_gamma):
    pos = jnp.arange(CHUNK, dtype=jnp.float32)
    dist = jnp.abs(pos[:, None] - pos[None, :])
    dmat = jnp.exp(log_gamma[:, None, None] * dist[None]).astype(qc.dtype)
    s = jnp.einsum('bnlhd,bnshd->bnhls', qc, kc) * dmat
    return jnp.einsum('bnhls,bnshe->bnlhe', s, vc)


def retention_cross_forward(qc, kc, vc, log_gamma):
    pos = jnp.arange(CHUNK, dtype=jnp.float32)
    k_dec = jnp.exp((CHUNK - 1 - pos)[:, None] * log_gamma[None, :]).astype(kc.dtype)
    q_dec = jnp.exp((pos + 1)[:, None] * log_gamma[None, :]).astype(qc.dtype)
    states = jnp.einsum('bnshd,sh,bnshe->bnhde', kc, k_dec, vc)
    b, n = states.shape[:2]
    chunk_dec = jnp.broadcast_to(jnp.exp(CHUNK * log_gamma).astype(states.dtype)[None, None, :, None, None],
                                 (b, n, RET_HEADS, 1, 1))
    prev = exclusive_chunk_scan(states, chunk_dec)
    return jnp.einsum('bnlhd,lh,bnhde->bnlhe', qc, q_dec, prev)


def bidirectional_retention(q, k, v, log_gamma):
    qc, kc, vc = to_chunks(q), to_chunks(k), to_chunks(v)
    y = retention_intra(qc, kc, vc, log_gamma) + retention_cross_forward(qc, kc, vc, log_gamma)
    qr, kr, vr = to_chunks(flip_seq(q)), to_chunks(flip_seq(k)), to_chunks(flip_seq(v))
    y_back = flip_seq(from_chunks(retention_cross_forward(qr, kr, vr, log_gamma)))
    return from_chunks(y) + y_back


def ssd_scan_forward(x, dt, a, bm, cm):
    b, lp = x.shape[:2]
    n = lp // CHUNK
    xd = (x * dt[..., None]).reshape(b, n, CHUNK, SSD_GROUPS, SSD_HPG, SSD_HEAD_DIM)
    acs = jnp.cumsum((dt * a).astype(jnp.float32).reshape(b, n, CHUNK, SSD_GROUPS, SSD_HPG), axis=2)
    bc = to_chunks(bm)
    cc = to_chunks(cm)
    diff = acs[:, :, :, None] - acs[:, :, None, :]
    tri = jnp.tril(jnp.ones((CHUNK, CHUNK), dtype=bool))[:, :, None, None]
    lmat = jnp.exp(jnp.where(tri, diff, -jnp.inf)).astype(x.dtype)
    cb = jnp.einsum('bclgn,bcsgn->bclsg', cc, bc)
    y_intra = jnp.einsum('bclsgh,bcsghp->bclghp', cb[..., None] * lmat, xd)
    decay_end = jnp.exp(acs[:, :, -1:] - acs).astype(x.dtype)
    states = jnp.einsum('bcsgn,bcsgh,bcsghp->bcghpn', bc, decay_end, xd)
    chunk_dec = jnp.exp(acs[:, :, -1]).astype(x.dtype)[..., None, None]
    prev = exclusive_chunk_scan(states, chunk_dec)
    y_off = jnp.einsum('bclgn,bclgh,bcghpn->bclghp', cc, jnp.exp(acs).astype(x.dtype), prev)
    return (y_intra + y_off).reshape(b, lp, SSD_HEADS, SSD_HEAD_DIM)


def hybrid_layer(h, pos, norm_mix_w, w_in, ret_gn_w, w_ret_out, w_ssd_conv, b_ssd_conv,
                 dt_bias_f, dt_bias_b, a_log_f, a_log_b, d_skip, ssd_norm_w, w_ssd_out, w_out,
                 norm_ffn_w, w_ffn_up, w_ffn_conv, b_ffn_conv, w_ffn_down):
    b, l, _ = h.shape
    u = rms_norm(h, norm_mix_w)
    proj = u @ w_in
    q, k, v, g_ret, z, xbc, dt_f, dt_b, gate_ret, gate_ssd = _split(proj, IN_SIZES)

    log_gamma = jnp.log(1.0 - 2.0 ** (-5.0 - jnp.arange(RET_HEADS, dtype=jnp.float32)))
    q = rotary(q.reshape(b, l, RET_HEADS, RET_QK_DIM), pos)
    k = rotary(k.reshape(b, l, RET_HEADS, RET_QK_DIM), pos) * (RET_QK_DIM ** -0.5)
    v = v.reshape(b, l, RET_HEADS, RET_V_DIM)
    y_ret = bidirectional_retention(pad_front(q), pad_front(k), pad_front(v), log_gamma)[:, PAD:]
    y_ret = head_group_norm(y_ret).reshape(b, l, RET_V) * ret_gn_w
    y_ret = (jax.nn.silu(g_ret) * y_ret) @ w_ret_out

    xbc = jax.nn.silu(dw_conv_centred(xbc, w_ssd_conv, b_ssd_conv))
    xs, bm, cm = _split(xbc, (SSD_D_INNER, SSD_GROUPS * SSD_STATE, SSD_GROUPS * SSD_STATE))
    xs = xs.reshape(b, l, SSD_HEADS, SSD_HEAD_DIM)
    bm = bm.reshape(b, l, SSD_GROUPS, SSD_STATE)
    cm = cm.reshape(b, l, SSD_GROUPS, SSD_STATE)
    dtf = jax.nn.softplus(dt_f + dt_bias_f)
    dtb = jax.nn.softplus(dt_b + dt_bias_b)
    a_f = -jnp.exp(a_log_f.astype(jnp.float32))
    a_b = -jnp.exp(a_log_b.astype(jnp.float32))
    xp, bp, cp = pad_front(xs), pad_front(bm), pad_front(cm)
    y_f = ssd_scan_forward(xp, pad_front(dtf), a_f, bp, cp)
    y_b = flip_seq(ssd_scan_forward(flip_seq(xp), flip_seq(pad_front(dtb)), a_b, flip_seq(bp), flip_seq(cp)))
    y = (y_f + y_b)[:, PAD:] + xs * d_skip[:, None]
    y = y.reshape(b, l, SSD_D_INNER) * jax.nn.silu(z)
    y_ssd = group_rms_norm(y, ssd_norm_w, SSD_GROUPS) @ w_ssd_out

    merged = jax.nn.sigmoid(gate_ret) * y_ret + jax.nn.sigmoid(gate_ssd) * y_ssd
    h = h + merged @ w_out

    f = dw_conv_centred(rms_norm(h, norm_ffn_w) @ w_ffn_up, w_ffn_conv, b_ffn_conv)
    fg, fu = _split(f, (D_FF, D_FF))
    return h + (jax.nn.silu(fg) * fu) @ w_ffn_down


def setup_inputs(seed: int = 0) -> dict:
    key = jax.random.key(seed)
    ks = jax.random.split(key, 24)
    f32 = jnp.float32

    def nrm(k, shape, scale):
        return jax.random.normal(k, shape, f32) * scale

    dt0 = jnp.exp(jax.random.uniform(ks[8], (2, DEPTH, SSD_HEADS), f32, minval=np.log(1e-3), maxval=np.log(1e-1)))
    dt_bias = dt0 + jnp.log(-jnp.expm1(-dt0))
    a_log = jnp.log(jax.random.uniform(ks[9], (2, DEPTH, SSD_HEADS), f32, minval=1.0, maxval=16.0))
    return {
        'x': nrm(ks[0], (BATCH, SEQ, D_MODEL), 1.0),
        'meta_tokens': nrm(ks[1], (N_META, D_MODEL), 1.0),
        'norm_mix_w': 1.0 + nrm(ks[2], (DEPTH, D_MODEL), 0.02),
        'w_in': nrm(ks[3], (DEPTH, D_MODEL, D_IN), D_MODEL ** -0.5),
        'ret_gn_w': 1.0 + nrm(ks[4], (DEPTH, RET_V), 0.02),
        'w_ret_out': nrm(ks[5], (DEPTH, RET_V, D_MODEL), RET_V ** -0.5),
        'w_ssd_conv': nrm(ks[6], (DEPTH, SSD_CONV, SSD_XBC), SSD_CONV ** -0.5),
        'b_ssd_conv': nrm(ks[7], (DEPTH, SSD_XBC), 0.02),
        'dt_bias_f': dt_bias[0],
        'dt_bias_b': dt_bias[1],
        'a_log_f': a_log[0],
        'a_log_b': a_log[1],
        'd_skip': 1.0 + nrm(ks[10], (DEPTH, SSD_HEADS), 0.02),
        'ssd_norm_w': 1.0 + nrm(ks[11], (DEPTH, SSD_D_INNER), 0.02),
        'w_ssd_out': nrm(ks[12], (DEPTH, SSD_D_INNER, D_MODEL), SSD_D_INNER ** -0.5),
        'w_out': nrm(ks[13], (DEPTH, D_MODEL, D_MODEL), D_MODEL ** -0.5),
        'norm_ffn_w': 1.0 + nrm(ks[14], (DEPTH, D_MODEL), 0.02),
        'w_ffn_up': nrm(ks[15], (DEPTH, D_MODEL, 2 * D_FF), D_MODEL ** -0.5),
        'w_ffn_conv': nrm(ks[16], (DEPTH, FFN_CONV, 2 * D_FF), FFN_CONV ** -0.5),
        'b_ffn_conv': nrm(ks[17], (DEPTH, 2 * D_FF), 0.02),
        'w_ffn_down': nrm(ks[18], (DEPTH, D_FF, D_MODEL), D_FF ** -0.5),
        'final_norm_w': 1.0 + nrm(ks[19], (D_MODEL,), 0.02),
    }


def reference(x, meta_tokens, norm_mix_w, w_in, ret_gn_w, w_ret_out, w_ssd_conv, b_ssd_conv,
              dt_bias_f, dt_bias_b, a_log_f, a_log_b, d_skip, ssd_norm_w, w_ssd_out, w_out,
              norm_ffn_w, w_ffn_up, w_ffn_conv, b_ffn_conv, w_ffn_down, final_norm_w):
    b = x.shape[0]
    meta = jnp.broadcast_to(meta_tokens[None].astype(x.dtype), (b, N_META, D_MODEL))
    h = jnp.concatenate([meta, x], axis=1)
    pos = jnp.arange(h.shape[1])
    for i in range(DEPTH):
        h = hybrid_layer(h, pos, norm_mix_w[i], w_in[i], ret_gn_w[i], w_ret_out[i], w_ssd_conv[i], b_ssd_conv[i],
                         dt_bias_f[i], dt_bias_b[i], a_log_f[i], a_log_b[i], d_skip[i], ssd_norm_w[i], w_ssd_out[i],
                         w_out[i], norm_ffn_w[i], w_ffn_up[i], w_ffn_conv[i], b_ffn_conv[i], w_ffn_down[i])
    h = rms_norm(h, final_norm_w)
    return h[:, N_META:]
```

```python
import numpy as np
from contextlib import ExitStack
import concourse.bass as bass
import concourse.mybir as mybir
from concourse.bass_utils import run_bass_kernel_spmd

AF = mybir.ActivationFunctionType
ALU = mybir.AluOpType
F32 = mybir.dt.float32
BF16 = mybir.dt.bfloat16
AX = mybir.AxisListType

D = 1024
SEQ = 4096
NMETA = 16
CH = 128
PAD = CH - NMETA
LP = PAD + NMETA + SEQ
NCH = LP // CH
RH = 4
SH = 32
SG = 4
DFF = 2816
EPS = 1e-6
DIN = 10304
OQ, OK_, OV, OG, OZ, OX, ODF, ODB, OGR, OGS = 0, 512, 1024, 2048, 3072, 5120, 8192, 8224, 8256, 9280


class Buf:
    __slots__ = ("name", "lw", "rd")

    def __init__(self, name):
        self.name = name
        self.lw = None
        self.rd = []


class Prog:
    ENG = ["pe", "act", "dve", "pool", "sp"]
    DMAQ = {"sp": 12, "pool": 6, "act": 4}

    def __init__(self):
        self.ops = {e: [] for e in self.ENG}
        self.n = {e: 0 for e in self.ENG}
        self.nd = {q: 0 for q in self.DMAQ}
        self.seen = {e: {} for e in self.ENG}
        self.bufs = {}
        self.pending = {e: {} for e in self.ENG}
        self.cap = None

    def begin_capture(self):
        self.cap = []

    def end_capture(self):
        l, self.cap = self.cap, None
        return l

    def replay(self, lst):
        for (kind, a, fn, r, w) in lst:
            if kind == "op":
                self.op(a, fn, r, w)
            else:
                self.dma(a, fn, r, w)

    def barrier(self):
        snap = {}
        for e in ["pe", "act", "dve", "pool"]:
            if self.n[e] > 0:
                snap[e] = self.n[e]
        for q, ns in self.DMAQ.items():
            i = self.nd[q]
            for slot in range(min(ns, i)):
                snap[("dma", q, slot)] = 16 * ((i - 1 - slot) // ns + 1)
        for e in self.ENG:
            d = self.pending[e]
            for k, v in snap.items():
                if k == e and e in ("pe", "act", "dve", "pool"):
                    continue
                if v > d.get(k, 0):
                    d[k] = v

    def buf(self, name):
        b = self.bufs.get(name)
        if b is None:
            b = self.bufs[name] = Buf(name)
        return b

    def _deps(self, eng, r, w, dma):
        waits = dict(self.pending[eng])
        self.pending[eng] = {}

        def need(dep, war=False):
            if dep is None:
                return
            key, val = dep
            if key == eng and not dma:
                if eng == "pe" or war:
                    return
            if val > waits.get(key, 0):
                waits[key] = val
        for b in r:
            need(b.lw)
        for b in w:
            need(b.lw)
            for d in b.rd:
                need(d, True)
        out = []
        for k, v in waits.items():
            if v > self.seen[eng].get(k, 0):
                self.seen[eng][k] = v
                out.append((k, v))
        return out

    def _mark(self, tok, r, w):
        for b in r:
            b.rd.append(tok)
        for b in w:
            b.lw = tok
            b.rd = []

    def op(self, eng, fn, r=(), w=()):
        if self.cap is not None:
            self.cap.append(("op", eng, fn, list(r), list(w)))
            return
        r = [self.buf(x) if isinstance(x, str) else x for x in r]
        w = [self.buf(x) if isinstance(x, str) else x for x in w]
        waits = self._deps(eng, r, w, False)
        self.n[eng] += 1
        tok = (eng, self.n[eng])
        self._mark(tok, r, w)
        self.ops[eng].append((waits, fn, (eng, 1)))

    def dma(self, q, fn, r=(), w=()):
        if self.cap is not None:
            self.cap.append(("dma", q, fn, list(r), list(w)))
            return
        r = [self.buf(x) if isinstance(x, str) else x for x in r]
        w = [self.buf(x) if isinstance(x, str) else x for x in w]
        waits = self._deps(q, r, w, True)
        i = self.nd[q]
        self.nd[q] += 1
        ns = self.DMAQ[q]
        slot, rnd = i % ns, i // ns
        key = ("dma", q, slot)
        if rnd > 0 and 16 * rnd > self.seen[q].get(key, 0):
            self.seen[q][key] = 16 * rnd
            waits.append((key, 16 * rnd))
        tok = (key, 16 * (rnd + 1))
        self._mark(tok, r, w)
        self.ops[q].append((waits, fn, (key, 16)))

    def emit(self, nc, final_waits_engine="sp"):
        keys = set()
        for e in self.ENG:
            for waits, fn, (k, inc) in self.ops[e]:
                keys.add(k)
        with ExitStack() as st:
            sems = {}
            for k in sorted(keys, key=str):
                nm = "s_" + ("_".join(str(x) for x in k) if isinstance(k, tuple) else k)
                sems[k] = st.enter_context(nc.semaphore(nm))
            block = st.enter_context(nc.Block())
            finals = {}
            for e in self.ENG:
                for waits, fn, (k, inc) in self.ops[e]:
                    finals[k] = finals.get(k, 0) + inc

            def run(e, engh):
                for waits, fn, (k, inc) in self.ops[e]:
                    for (wk, wv) in waits:
                        engh.wait_ge(sems[wk], wv)
                    ins = fn(engh)
                    ins.then_inc(sems[k], inc)
                if e == final_waits_engine:
                    for k, v in finals.items():
                        engh.wait_ge(sems[k], v)

            block.tensor(lambda eh: run("pe", eh))
            block.scalar(lambda eh: run("act", eh))
            block.vector(lambda eh: run("dve", eh))
            block.gpsimd(lambda eh: run("pool", eh))
            block.sync(lambda eh: run("sp", eh))


def merge_threads(lists, offs, spans):
    items = []
    for i, L in enumerate(lists):
        n = len(L)
        for j, it in enumerate(L):
            items.append((offs[i] + spans[i] * (j + 0.5) / n, i, j, it))
    items.sort(key=lambda t: (t[0], t[1], t[2]))
    return [t[3] for t in items]


GAM = [1.0 - 2.0 ** (-5.0 - h) for h in range(RH)]
NEGBIG = -30000.0


def _consts():
    c = {}
    f32 = np.float32
    c["c_ident"] = np.eye(128, dtype=f32)
    half = 64
    inv = (10000.0 ** (-np.arange(half, dtype=np.float64) / half))
    pos = np.arange(LP, dtype=np.float64) - PAD
    ang = (pos[None, :] * inv[:, None]).astype(f32)
    ang = (pos.astype(f32)[None, :] * inv.astype(f32)[:, None]).astype(f32)
    cos = np.cos(ang).astype(f32)
    sin = np.sin(ang).astype(f32)
    c["c_cos"] = np.concatenate([cos, cos], 0)
    c["c_sin"] = np.concatenate([-sin, sin], 0)
    i = np.arange(128)
    s_, l_ = i[:, None], i[None, :]
    g = np.array(GAM, dtype=np.float64)
    sc = 128.0 ** -0.5
    dm = np.stack([g[h] ** np.abs(l_ - s_) * sc for h in range(RH)], 1)
    c["c_dmT"] = dm.astype(f32)
    c["c_qdf"] = np.stack([np.broadcast_to(g[h] ** (l_ + 1.0), (128, 128)) for h in range(RH)], 1).astype(f32)
    c["c_qdb"] = np.stack([np.broadcast_to(g[h] ** (128.0 - l_), (128, 128)) for h in range(RH)], 1).astype(f32)
    kd = np.zeros((128, 8), f32)
    for h in range(RH):
        kd[:, h] = g[h] ** (127.0 - i) * sc
        kd[:, 4 + h] = g[h] ** (i * 1.0) * sc
    c["c_kdec"] = kd
    tri_f = (s_ <= l_).astype(f32)
    c["c_tri"] = np.stack([tri_f, tri_f.T.copy(), (s_ > l_).astype(f32), (s_ < l_).astype(f32),
                           np.ones((128, 128), f32)], 1)
    c["c_mask"] = np.stack([(l_ >= s_).astype(f32), (l_ <= s_).astype(f32)], 1)
    nf = np.where(l_ < s_, NEGBIG, 0.0).astype(f32)
    nb = np.where(l_ > s_, NEGBIG, 0.0).astype(f32)
    c["c_negm"] = np.stack([np.tile(nf, (1, 4)), np.tile(nb, (1, 4))], 1)
    sel = np.zeros((96, 32, 128), f32)
    for r in range(96):
        sel[r, r % 32, :] = 1.0
    c["c_sel"] = sel.reshape(96, 4096)
    return c


def build(dbg=False):
    nc = bass.Bass("TRN2", target_bir_lowering=False)
    P = Prog()

    def din(name, shape, dt=F32):
        return nc.dram_tensor(name, list(shape), dt, kind="ExternalInput").ap()

    def dscr(name, shape, dt):
        return nc.dram_tensor(name, list(shape), dt, kind=("ExternalOutput" if dbg else "Internal")).ap()

    x_in = din("x", [SEQ, D])
    meta_in = din("meta", [NMETA, D])
    w_in = din("w_in", [D, DIN])
    w_ret_out = din("w_ret_out", [1024, D])
    w_ssd_out = din("w_ssd_out", [2048, D])
    w_out = din("w_out", [D, D])
    w_up = din("w_ffn_up", [D, 2 * DFF])
    w_down = din("w_ffn_down", [DFF, D])
    r_nmw = din("r_nmw", [D]); r_nfw = din("r_nfw", [D]); r_fnw = din("r_fnw", [D])
    r_dtb = din("r_dtb", [64]); r_alog = din("r_alog", [64]); r_dskip = din("r_dskip", [32])
    p_scw = din("p_scw", [128, 72]); p_scb = din("p_scb", [128, 24])
    p_fcw = din("p_fcw", [128, 132]); p_fcb = din("p_fcb", [128, 44])
    p_ncol = din("p_ncol", [128, 24])
    cin = {k: din(k, v.shape) for k, v in _consts().items()}
    out = nc.dram_tensor("out", [SEQ, D], F32, kind="ExternalOutput").ap()

    QT = dscr("QT", [4, 128, LP], BF16); KT = dscr("KT", [4, 128, LP], BF16)
    BT = dscr("BT", [4, 128, LP], BF16); CT = dscr("CT", [4, 128, LP], BF16)
    K_tm = dscr("K_tm", [LP, 512], BF16); V_tm = dscr("V_tm", [LP, 1024], BF16)
    XS_tm = dscr("XS_tm", [LP, 2048], BF16); B_tm = dscr("B_tm", [LP, 512], BF16)
    DT_tm = dscr("DT_tm", [LP, 64], F32)
    Z_tm = dscr("Z_tm", [LP, 2048], BF16); G_tm = dscr("G_tm", [LP, 1024], BF16)
    GG_tm = dscr("GG_tm", [LP, 2048], BF16)
    RFs = dscr("RFs", [NCH, 128, 1024], BF16); PFs = dscr("PFs", [NCH, 128, 2048], BF16)
    YRT = dscr("YRT", [NCH, 128, 8, 128], BF16); YST = dscr("YST", [NCH, 128, 16, 128], BF16)
    H2 = dscr("H2", [LP, D], F32)
    WUPb = dscr("WUPb", [128, 8, 2 * DFF], BF16)

    def TT(eng, o, a, b, op, r, w):
        P.op(eng, lambda e: e.tensor_tensor(out=o, in0=a, in1=b, op=op), r, w)

    def TS(eng, o, a, s1, s2, op0, op1, r, w):
        if op1 is None:
            P.op(eng, lambda e: e.tensor_scalar(out=o, in0=a, scalar1=s1, scalar2=None, op0=op0), r, w)
        else:
            P.op(eng, lambda e: e.tensor_scalar(out=o, in0=a, scalar1=s1, scalar2=s2, op0=op0, op1=op1), r, w)

    def STT(o, a, sc, b, op0, op1, r, w):
        P.op("dve", lambda e: e.scalar_tensor_tensor(out=o, in0=a, scalar=sc, in1=b, op0=op0, op1=op1), r, w)

    def ACTV(o, a, func, r, w, scale=1.0, bias=0.0, accum=None):
        if accum is None:
            P.op("act", lambda e: e.activation(out=o, in_=a, func=func, scale=scale, bias=bias), r, w)
        else:
            P.op("act", lambda e: e.activation(out=o, in_=a, func=func, scale=scale, bias=bias, accum_out=accum), r, w)

    def CP(eng, o, a, r, w):
        if eng == "act":
            P.op("act", lambda e: e.copy(out=o, in_=a), r, w)
        else:
            P.op(eng, lambda e: e.tensor_copy(out=o, in_=a), r, w)

    def MM(lst, r, w):
        def f(e):
            ins = None
            for (o, lt, rh, st_, sp_) in lst:
                ins = e.matmul(o, lt, rh, start=st_, stop=sp_)
            return ins
        P.op("pe", f, r, w)

    def TR(lst, r, w):
        def f(e):
            ins = None
            for (o, a) in lst:
                ins = e.transpose(out=o, in_=a, identity=ident[:])
            return ins
        P.op("pe", f, list(r) + ["ident"], w)

    def DMA(o, a, r, w, q="sp"):
        P.dma(q, lambda e: e.dma_start(out=o, in_=a), r, w)

    def MSET(eng, o, v, w):
        P.op(eng, lambda e: e.memset(o, v), [], w)

    top = ExitStack()
    with top:
        def mk(stack):
            def sb(name, shape, dt=F32):
                return stack.enter_context(nc.sbuf_tensor(name, list(shape), dt))
            return sb
        sbT = mk(top)
        PBK = [top.enter_context(nc.psum_tensor(f"bank{i}", [128, 512], F32)) for i in range(8)]
        bank_rr = [0]

        def getbank(lo=0, hi=8):
            b = lo + bank_rr[0] % (hi - lo)
            bank_rr[0] += 1
            return b

        def bk(b):
            return PBK[b][:]

        def bkbf(b):
            return PBK[b][:].bitcast(BF16).rearrange("p (a b) -> p a b", a=8)

        ident = sbT("ident", [128, 128], BF16)
        stg = sbT("cstage", [128, 4224], F32)
        DMA(stg[:, 0:128], cin["c_ident"], [], ["cstage"])
        CP("dve", ident[:], stg[:, 0:128], ["cstage"], ["ident"])

        def rowbc(name, src, n):
            t = sbT(name, [128, n], F32)
            DMA(t[:], src.partition_broadcast(128), [], [name])
            return t

        s12 = ExitStack()
        sb12 = mk(s12)
        uT = sb12("uT", [128, 8, LP + 2], BF16)
        s1 = ExitStack()
        sb1 = mk(s1)
        wbc = sb1("wbc", [128, D])
        DMA(wbc[:], r_nmw.partition_broadcast(128), [], ["wbc"])
        junk = sb1("junk", [128, D])
        MSET("pool", uT[:, :, 0:1], 0.0, ["uT_h0"])
        MSET("pool", uT[:, :, LP + 1:LP + 2], 0.0, ["uT_h1"])

        def rmsnorm_to_T(X, bx, wtile, bw, dstT, c, bdst, U, bu, SS, bss, RS, brs, junk_t, bjunk, u2name=None):
            if u2name is not None:
                bu = u2name
            ACTV(junk_t if u2name is not None else junk_t[:], X[:], AF.Square, bx, [bjunk, bss], accum=SS[:])
            ACTV(RS[:], SS[:], AF.Sqrt, [bss], [brs], scale=1.0 / D, bias=EPS)
            P.op("dve", lambda e: e.reciprocal(out=RS[:], in_=RS[:]), [brs], [brs])
            STT(U[:], X[:], RS[:], wtile[:], ALU.mult, ALU.mult, list(bx) + [brs, bw], [bu])
            b = getbank()
            TR([(bkbf(b)[:, k, :], U[:, k * 128:(k + 1) * 128]) for k in range(8)], [bu], [f"bank{b}"])
            CP("act", dstT[:, :, 1 + c * 128:1 + (c + 1) * 128], bkbf(b), [f"bank{b}"], [bdst])

        xt = [sb1(f"xt{i}", [128, D]) for i in range(2)]
        ub = [sb1(f"ub{i}", [128, D], BF16) for i in range(2)]
        ssq = [sb1(f"ss{i}", [128, 1]) for i in range(2)]
        rsq = [sb1(f"rs{i}", [128, 1]) for i in range(2)]
        for c in range(NCH):
            i = c % 2
            if c == 0:
                MSET("pool", xt[i][:], 0.0, [f"xt{i}"])
                DMA(xt[i][PAD:128, :], meta_in, [], [f"xt{i}"])
            else:
                DMA(xt[i][:], x_in[(c - 1) * 128:c * 128, :], [], [f"xt{i}"])
            rmsnorm_to_T(xt[i], [f"xt{i}"], wbc, "wbc", uT, c, f"uT_{c}", ub[i], f"ub{i}", ssq[i], f"ss{i}",
                         rsq[i], f"rs{i}", junk, "junk")
        uT_all = [f"uT_{c}" for c in range(NCH)] + ["uT_h0", "uT_h1"]
        P.barrier()
        s1.close()

        s2 = ExitStack()
        sb2 = mk(s2)
        cosT = sb2("cosT", [128, LP]); sinT = sb2("sinT", [128, LP])
        DMA(cosT[:], cin["c_cos"], [], ["cosT"]); DMA(sinT[:], cin["c_sin"], [], ["sinT"])
        scw = sb2("scw", [128, 72]); scb = sb2("scb", [128, 24])
        DMA(scw[:], p_scw, [], ["scw"]); DMA(scb[:], p_scb, [], ["scb"])
        dtb_bc = sb2("dtb_bc", [128, 64])
        DMA(dtb_bc[:], r_dtb.partition_broadcast(128), [], ["dtb_bc"])
        wst = [sb2(f"wst{i}", [128, 8, 512]) for i in range(2)]
        wbf = [sb2(f"wbf{i}", [128, 8, 512], BF16) for i in range(2)]
        wctr = [0]

        wupf = [sb2("wupf0", [128, 2816])]
        wupst = [sb2("wupst0", [128, 2816], BF16)]
        WUPv = WUPb.rearrange("p k (j t c) -> p k j t c", j=22, t=2)

        def wup_iter(idx):
            k, t = idx // 2, idx % 2
            wi = 0
            DMA(wupf[wi][:], w_up[k * 128:(k + 1) * 128, t * 2816:(t + 1) * 2816], [], [f"wupf{wi}"])
            CP("pool", wupst[wi][:], wupf[wi][:], [f"wupf{wi}"], [f"wupst{wi}"])
            for (j0, j1) in [(0, 6), (6, 12), (12, 17), (17, 22)]:
                DMA(WUPv[:, k, j0:j1, t, :], wupst[wi][:].rearrange("p (a b) -> p a b", a=22)[:, j0:j1, :],
                    [f"wupst{wi}"], [f"WUPb{j0}"])

        def load_w(pieces, ncols):
            i = wctr[0] % 2
            if wctr[0] < 16:
                wup_iter(wctr[0])
            wctr[0] += 1
            for (d0, s0, n) in pieces:
                DMA(wst[i][:, :, d0:d0 + n], w_in[:, s0:s0 + n].rearrange("(k p) n -> p k n", p=128), [], [f"wst{i}"])
            CP("pool", wbf[i][:, :, 0:ncols], wst[i][:, :, 0:ncols], [f"wst{i}"], [f"wbf{i}"])
            return wbf[i], f"wbf{i}"

        t1 = [sb2(f"t1_{i}", [128, 512]) for i in range(2)]
        t2 = [sb2(f"t2_{i}", [128, 512]) for i in range(2)]
        NOB, NTRB, NOT, NDT = 5, 6, 6, 3
        ob = [sb2(f"ob{i}", [128, 512], BF16) for i in range(NOB)]
        trb = [sb2(f"trb{i}", [128, 4, 128], BF16) for i in range(NTRB)]
        ot = [sb2(f"ot{i}", [128, 512], BF16) for i in range(NOT)]
        dtt = [sb2(f"dtt{i}", [128, 64]) for i in range(NDT)]
        ctr = {"t": 0, "o": 0, "tr": 0, "ot": 0, "dt": 0}

        def transposes_to_tm(O, bo, n, dst_fn):
            nj = n // 128
            b = getbank()
            i = ctr["tr"] % NTRB
            ctr["tr"] += 1
            TR([(bkbf(b)[:, j, :], O[:, j * 128:(j + 1) * 128]) for j in range(nj)], [bo], [f"bank{b}"])
            CP("act", trb[i][:, 0:nj, :], bkbf(b)[:, 0:nj, :], [f"bank{b}"], [f"trb{i}"])
            ap, nm = dst_fn(nj)
            DMA(ap, trb[i][:, 0:nj, :], [f"trb{i}"], [nm])

        pend = []

        def flush():
            while pend:
                pend.pop(0)()

        jobs = []

        def qk_job(fam, DST, h):
            def body(W, bW):
                if stage2 or pend:
                    drain_job()(W, bW)
                for tb in range(9):
                    t0 = tb * 512
                    n = min(512, LP - t0)
                    ba, bb_ = getbank(), getbank()
                    MM([(bk(ba)[:, 0:n], W[:, k, 0:128], uT[:, k, 1 + t0:1 + t0 + n], k == 0, k == 7) for k in range(8)],
                       [bW] + uT_all, [f"bank{ba}"])
                    MM([(bk(bb_)[:, 0:n], W[:, k, 128:256], uT[:, k, 1 + t0:1 + t0 + n], k == 0, k == 7) for k in range(8)],
                       [bW] + uT_all, [f"bank{bb_}"])
                    flush()
                    i = ctr["t"] % 2
                    ctr["t"] += 1
                    io = ctr["o"] % NOB
                    ctr["o"] += 1
                    TT("dve", t1[i][:, 0:n], bk(ba)[:, 0:n], cosT[:, t0:t0 + n], ALU.mult, [f"bank{ba}", "cosT"], [f"t1_{i}"])
                    TT("dve", t2[i][:, 0:n], bk(bb_)[:, 0:n], sinT[:, t0:t0 + n], ALU.mult, [f"bank{bb_}", "sinT"], [f"t2_{i}"])
                    TT("pool" if tb % 3 == 2 else "dve", ob[io][:, 0:n], t1[i][:, 0:n], t2[i][:, 0:n], ALU.add,
                       [f"t1_{i}", f"t2_{i}"], [f"ob{io}"])
                    DMA(DST[h, :, t0:t0 + n], ob[io][:, 0:n], [f"ob{io}"], [f"{'QK'[fam]}T{h}_{tb}"])
                    if fam == 1:
                        pend.append(lambda io=io, n=n, t0=t0, h=h: transposes_to_tm(
                            ob[io], f"ob{io}", n,
                            lambda nj: (K_tm[t0:t0 + nj * 128, h * 128:(h + 1) * 128].rearrange("(j p) d -> p j d", p=128), "K_tm")))
            return body

        for fam, (col0, DST) in enumerate([(OQ, QT), (OK_, KT)]):
            for h in range(RH):
                c0 = col0 + h * 128
                jobs.append(([(0, c0, 128), (128, c0 + 64, 64), (192, c0, 64)], 256, qk_job(fam, DST, h)))

        stage2 = []

        def xbc_tail(m, tb, t0, i, io):
            T = t1[i]
            ACTV(ob[io][:, 0:384], T[:, 0:384], AF.Silu, [f"t1_{i}"], [f"ob{io}"])
            if tb == 0:
                MSET("pool", ob[io][:, 0:PAD], 0.0, [f"ob{io}"])
            flush()
            if m < 16:
                pend.append(lambda: transposes_to_tm(
                    ob[io], f"ob{io}", 384,
                    lambda nj: (XS_tm[t0:t0 + nj * 128, m * 128:(m + 1) * 128].rearrange("(j p) d -> p j d", p=128), "XS_tm")))
            elif m < 20:
                g = m - 16
                DMA(BT[g, :, t0:t0 + 384], ob[io][:, 0:384], [f"ob{io}"], [f"BT{g}_{tb}"])
                pend.append(lambda: transposes_to_tm(
                    ob[io], f"ob{io}", 384,
                    lambda nj: (B_tm[t0:t0 + nj * 128, g * 128:(g + 1) * 128].rearrange("(j p) d -> p j d", p=128), "B_tm")))
            else:
                g = m - 20
                DMA(CT[g, :, t0:t0 + 384], ob[io][:, 0:384], [f"ob{io}"], [f"CT{g}_{tb}"])

        def xbc_job(f4):
          def body(W, bW):
            for mm in range(4):
                m = f4 * 4 + mm
                for tb in range(11):
                    t0 = tb * 384
                    b = getbank()
                    MM([(bk(b)[:, 0:386], W[:, k, mm * 128:(mm + 1) * 128], uT[:, k, t0:t0 + 386], k == 0, k == 7) for k in range(8)],
                       [bW] + uT_all, [f"bank{b}"])
                    i = ctr["t"] % 2
                    ctr["t"] += 1
                    io = ctr["o"] % NOB
                    ctr["o"] += 1
                    T = t1[i]
                    ACTV(T[:, 0:384], bk(b)[:, 1:385], AF.Identity, [f"bank{b}", "scw", "scb"], [f"t1_{i}"],
                         scale=scw[:, m * 3 + 1:m * 3 + 2], bias=scb[:, m:m + 1])
                    STT(T[:, 0:384], bk(b)[:, 0:384], scw[:, m * 3:m * 3 + 1], T[:, 0:384], ALU.mult, ALU.add,
                        [f"bank{b}", f"t1_{i}"], [f"t1_{i}"])
                    STT(T[:, 0:384], bk(b)[:, 2:386], scw[:, m * 3 + 2:m * 3 + 3], T[:, 0:384], ALU.mult, ALU.add,
                        [f"bank{b}", f"t1_{i}"], [f"t1_{i}"])
                    while stage2:
                        stage2.pop(0)()
                    stage2.append(lambda m=m, tb=tb, t0=t0, i=i, io=io: xbc_tail(m, tb, t0, i, io))
          return body

        for f4 in range(6):
            jobs.append(([(0, OX + f4 * 512, 512)], 512, xbc_job(f4)))

        def drain_job():
            def body(W, bW):
                while stage2:
                    stage2.pop(0)()
                flush()
            return body
        def tm_family(col0, ncols, func, DST, dcol0, name):
          def body(W, bW):
            if stage2 or pend:
                drain_job()(W, bW)
            for c in range(NCH):
                b = getbank()
                MM([(bk(b)[:, 0:ncols], uT[:, k, 1 + c * 128:1 + (c + 1) * 128], W[:, k, 0:ncols], k == 0, k == 7) for k in range(8)],
                   [bW] + uT_all, [f"bank{b}"])
                i = ctr["ot"] % NOT
                ctr["ot"] += 1
                if func is None:
                    CP("act", ot[i][:, 0:ncols], bk(b)[:, 0:ncols], [f"bank{b}"], [f"ot{i}"])
                else:
                    ACTV(ot[i][:, 0:ncols], bk(b)[:, 0:ncols], func, [f"bank{b}"], [f"ot{i}"])
                DMA(DST[c * 128:(c + 1) * 128, dcol0:dcol0 + ncols], ot[i][:, 0:ncols], [f"ot{i}"], [name])
          jobs.append(([(0, col0, ncols)], ncols, body))

        for j in range(2):
            tm_family(OV + j * 512, 512, None, V_tm, j * 512, "V_tm")
        for j in range(2):
            tm_family(OG + j * 512, 512, AF.Silu, G_tm, j * 512, "G_tm")
        for j in range(4):
            tm_family(OZ + j * 512, 512, AF.Silu, Z_tm, j * 512, "Z_tm")
        for j in range(4):
            tm_family(OGR + j * 512, 512, AF.Sigmoid, GG_tm, j * 512, "GG_tm")
        def dt_body(W, bW):
          if stage2 or pend:
              drain_job()(W, bW)
          for c in range(NCH):
            b = getbank()
            MM([(bk(b)[:, 0:64], uT[:, k, 1 + c * 128:1 + (c + 1) * 128], W[:, k, 0:64], k == 0, k == 7) for k in range(8)],
               [bW] + uT_all, [f"bank{b}"])
            i = ctr["dt"] % NDT
            ctr["dt"] += 1
            TT("dve", dtt[i][:], bk(b)[:, 0:64], dtb_bc[:], ALU.add, [f"bank{b}", "dtb_bc"], [f"dtt{i}"])
            ACTV(dtt[i][:], dtt[i][:], AF.Exp, [f"dtt{i}"], [f"dtt{i}"])
            ACTV(dtt[i][:], dtt[i][:], AF.Ln, [f"dtt{i}"], [f"dtt{i}"], bias=1.0)
            if c == 0:
                MSET("pool", dtt[i][0:PAD, :], 0.0, [f"dtt{i}"])
            DMA(DT_tm[c * 128:(c + 1) * 128, :], dtt[i][:], [f"dtt{i}"], ["DT_tm"])
        jobs.append(([(0, ODF, 64)], 64, dt_body))
        assert len(jobs) == 27
        order = [0, 14, 1, 15, 2, 16, 3, 17, 4, 18, 5, 19, 6, 20, 7, 21, 8, 9, 10, 11, 12, 13, 22, 23, 24, 25, 26]
        jobs = [jobs[j] for j in order]
        loaded = {0: load_w(jobs[0][0], jobs[0][1])}
        for f in range(len(jobs)):
            if f + 1 < len(jobs):
                loaded[f + 1] = load_w(jobs[f + 1][0], jobs[f + 1][1])
            jobs[f][2](*loaded[f])
        drain_job()(None, None)
        P.barrier()
        s2.close()
        s12.close()
        if dbg == "p2":
            DMA(out[0:128, :], x_in[0:128, :], [], ["out"])
            P.emit(nc)
            return nc

        sFB = ExitStack()
        sbFB = mk(sFB)
        kdec = sbFB("kdec", [128, 8]); DMA(kdec[:], cin["c_kdec"], [], ["kdec"])
        tri = sbFB("tri", [128, 5, 128]); DMA(tri[:], cin["c_tri"], [], ["tri"])
        a_bc = sbFB("a_bc", [128, 64])
        DMA(a_bc[:], r_alog.partition_broadcast(128), [], ["a_bc"])
        ACTV(a_bc[:], a_bc[:], AF.Exp, ["a_bc"], ["a_bc"])
        TS("dve", a_bc[:], a_bc[:], -1.0, None, ALU.mult, None, ["a_bc"], ["a_bc"])
        CD = [g ** 128.0 for g in GAM]

        def v3(ap, a):
            return ap.rearrange("p (a b) -> p a b", a=a)

        def bc(ap2, n):
            return ap2.unsqueeze(2).to_broadcast([128, ap2.shape[1], n])

        sF = ExitStack()
        sbF = mk(sF)
        Rf = sbF("Rf", [128, 1024]); Sf = sbF("Sf", [128, 2048])
        Rf_bf = [sbF(f"Rf_bf{i}", [128, 1024], BF16) for i in range(2)]
        Sf_bf = [sbF(f"Sf_bf{i}", [128, 2048], BF16) for i in range(2)]
        MSET("pool", Rf[:], 0.0, ["Rf"]); MSET("pool", Sf[:], 0.0, ["Sf"])
        MSET("pool", Rf_bf[0][:], 0.0, ["Rf_bf0"]); MSET("pool", Sf_bf[0][:], 0.0, ["Sf_bf0"])
        fin = [dict(k=sbF(f"fk{i}", [128, 512], BF16), v=sbF(f"fv{i}", [128, 1024], BF16),
                    xs=sbF(f"fxs{i}", [128, 2048], BF16), b=sbF(f"fb{i}", [128, 512], BF16),
                    dt=sbF(f"fdt{i}", [128, 64])) for i in range(3)]
        kd = [sbF(f"pkd{i}", [128, 512], BF16) for i in range(3)]
        dta = [sbF(f"pdta{i}", [128, 32]) for i in range(3)]
        Ef = [sbF(f"pEf{i}", [128, 64]) for i in range(3)]
        w2 = [sbF(f"pw2{i}", [128, 32]) for i in range(3)]
        xdd = [sbF(f"pxdd{i}", [128, 2048], BF16) for i in range(3)]

        def pf_load(c):
            i = c % 3
            I = fin[i]
            rows = slice(c * 128, (c + 1) * 128)
            DMA(I["dt"][:], DT_tm[rows, :], ["DT_tm"], [f"fdt{i}"])
            DMA(I["k"][:], K_tm[rows, :], ["K_tm"], [f"fk{i}"])
            DMA(I["xs"][:], XS_tm[rows, :], ["XS_tm"], [f"fxs{i}"])
            DMA(I["v"][:], V_tm[rows, :], ["V_tm"], [f"fv{i}"])
            DMA(I["b"][:], B_tm[rows, :], ["B_tm"], [f"fb{i}"])

        def pf_pro(c):
            i = c % 3
            I = fin[i]
            TT("dve", v3(kd[i][:], 4), v3(I["k"][:], 4), bc(kdec[:, 0:4], 128), ALU.mult, [f"fk{i}", "kdec"], [f"pkd{i}"])
            TT("dve", dta[i][:], I["dt"][:, 0:32], a_bc[:, 0:32], ALU.mult, [f"fdt{i}", "a_bc"], [f"pdta{i}"])
            b = getbank()
            MM([(bk(b)[:, 0:32], tri[:, 4, :], dta[i][:], True, True), (bk(b)[:, 32:64], tri[:, 2, :], dta[i][:], True, True)],
               ["tri", f"pdta{i}"], [f"bank{b}"])
            ACTV(Ef[i][:], bk(b)[:, 0:64], AF.Exp, [f"bank{b}"], [f"pEf{i}"])
            TT("dve", w2[i][:], I["dt"][:, 0:32], Ef[i][:, 32:64], ALU.mult, [f"fdt{i}", f"pEf{i}"], [f"pw2{i}"])
            TT("pool", v3(xdd[i][:], 32), v3(I["xs"][:], 32), bc(w2[i][:], 64), ALU.mult, [f"fxs{i}", f"pw2{i}"], [f"pxdd{i}"])

        def pf_upd(c):
            i = c % 3
            j = c % 2
            I = fin[i]
            DMA(RFs[c], Rf_bf[j][:], [f"Rf_bf{j}"], [f"RFs{c}"])
            DMA(PFs[c], Sf_bf[j][:], [f"Sf_bf{j}"], [f"PFs{c}"])
            if c == NCH - 1:
                return
            TT("dve", v3(Sf[:], 32), v3(Sf[:], 32), bc(Ef[i][:, 0:32], 64), ALU.mult, ["Sf", f"pEf{i}"], ["Sf"])
            for hp in range(2):
                b = getbank()
                MM([(bk(b)[:, hh * 256:(hh + 1) * 256], kd[i][:, (2 * hp + hh) * 128:(2 * hp + hh + 1) * 128],
                     I["v"][:, (2 * hp + hh) * 256:(2 * hp + hh + 1) * 256], True, True) for hh in range(2)],
                   [f"pkd{i}", f"fv{i}"], [f"bank{b}"])
                for hh in range(2):
                    h = 2 * hp + hh
                    STT(Rf[:, h * 256:(h + 1) * 256], Rf[:, h * 256:(h + 1) * 256], CD[h], bk(b)[:, hh * 256:(hh + 1) * 256],
                        ALU.mult, ALU.add, ["Rf", f"bank{b}"], ["Rf"])
            CP("act", Rf_bf[1 - j][:], Rf[:], ["Rf"], [f"Rf_bf{1 - j}"])
            for g in range(4):
                b = getbank()
                MM([(bk(b), I["b"][:, g * 128:(g + 1) * 128], xdd[i][:, g * 512:(g + 1) * 512], True, True)],
                   [f"fb{i}", f"pxdd{i}"], [f"bank{b}"])
                sg_ = Sf[:, g * 512:(g + 1) * 512]
                TT("dve", sg_, sg_, bk(b), ALU.add, ["Sf", f"bank{b}"], ["Sf"])
            CP("act", Sf_bf[1 - j][:], Sf[:], ["Sf"], [f"Sf_bf{1 - j}"])

        pf_load(0)
        pf_load(1)
        pf_pro(0)
        pf_pro(1)
        for c in range(NCH):
            if c + 2 < NCH - 1:
                pf_load(c + 2)
                pf_pro(c + 2)
            pf_upd(c)
        P.barrier()
        sF.close()
        if dbg == "pf":
            DMA(out[0:128, :], x_in[0:128, :], [], ["out"])
            P.emit(nc)
            return nc

        sB = ExitStack()
        sbB = mk(sB)
        dmT = sbB("dmT", [128, 4, 128]); DMA(dmT[:], cin["c_dmT"], [], ["dmT"])
        qdf = sbB("qdf", [128, 4, 128]); DMA(qdf[:], cin["c_qdf"], [], ["qdf"])
        qdb = sbB("qdb", [128, 4, 128]); DMA(qdb[:], cin["c_qdb"], [], ["qdb"])
        mask = sbB("mask", [128, 2, 128]); DMA(mask[:], cin["c_mask"], [], ["mask"])
        negm = sbB("negm", [128, 2, 512], BF16)
        DMA(stg[:, 0:1024], cin["c_negm"].rearrange("p a b -> p (a b)"), ["cstage"], ["cstage"])
        CP("dve", negm[:].rearrange("p a b -> p (a b)"), stg[:, 0:1024], ["cstage"], ["negm"])
        sel = sbB("sel", [96, 4096], BF16)
        DMA(stg[0:96, 0:4096], cin["c_sel"], ["cstage"], ["cstage"])
        CP("dve", sel[:], stg[0:96, 0:4096], ["cstage"], ["sel"])
        dsk_bc = sbB("dsk_bc", [128, 32]); DMA(dsk_bc[:], r_dskip.partition_broadcast(128), [], ["dsk_bc"])
        Rb = sbB("Rb", [128, 1024]); Sb = sbB("Sb", [128, 2048])
        Rb_bf = sbB("Rb_bf", [128, 1024], BF16); Sb_bf = sbB("Sb_bf", [128, 2048], BF16)
        MSET("pool", Rb[:], 0.0, ["Rb"]); MSET("pool", Sb[:], 0.0, ["Sb"])
        MSET("pool", Rb_bf[:], 0.0, ["Rb_bf"]); MSET("pool", Sb_bf[:], 0.0, ["Sb_bf"])
        rinp = [dict(qT=sbB(f"bqT{i}", [128, 4, 128], BF16), kT=sbB(f"bkT{i}", [128, 4, 128], BF16),
                     k=sbB(f"bk{i}", [128, 512], BF16), v=sbB(f"bv{i}", [128, 1024], BF16),
                     g=sbB(f"bg{i}", [128, 1024], BF16), rf=sbB(f"brf{i}", [128, 1024], BF16)) for i in range(2)]
        sinp = [dict(xs=sbB(f"bxs{i}", [128, 2048], BF16), bT=sbB(f"bbT{i}", [128, 4, 128], BF16),
                     cT=sbB(f"bcT{i}", [128, 4, 128], BF16), b=sbB(f"bb{i}", [128, 512], BF16),
                     dt=sbB(f"bdt{i}", [128, 64]),
                     pf=sbB(f"bpf{i}", [128, 2048], BF16)) for i in range(2)]
        zin = [sbB(f"bz{i}", [128, 2048], BF16) for i in range(3)]
        SD = sbB("SD", [128, 4, 128], BF16); qf = sbB("qf", [128, 4, 128], BF16); qb = sbB("qb", [128, 4, 128], BF16)
        yr = sbB("yr", [128, 1024]); kdb = sbB("kdb", [128, 512], BF16)
        st6 = sbB("st6", [128, 4, 6]); mv = sbB("mv", [128, 4, 2]); rstd = sbB("rstd", [128, 4]); lnt = sbB("lnt", [128, 4])
        lnt2 = sbB("lnt2", [128, 4])
        yg = sbB("yg", [128, 1024], BF16); ygT = sbB("ygT", [128, 8, 128], BF16)
        dta2 = sbB("dta2", [128, 64])
        acs = [sbB(f"acs{i}", [128, 64]) for i in range(2)]
        E = [sbB(f"E{i}", [128, 160]) for i in range(2)]
        A3 = sbB("A3", [128, 2, 96], BF16); r1 = sbB("r1", [128, 2, 32]); r2 = sbB("r2", [128, 2, 32])
        aT3 = [sbB(f"aT3_{i}", [96, 2, 128], BF16) for i in range(2)]
        naT3 = [sbB(f"naT3_{i}", [96, 2, 128], BF16) for i in range(2)]
        cbm = [sbB(f"cbm{i}", [128, 2, 512], BF16) for i in range(2)]
        xd = [[sbB(f"xd{i}_{d}", [128, 2048], BF16) for d in range(2)] for i in range(2)]
        xsd = [sbB(f"xsd{i}", [128, 2048], BF16) for i in range(2)]
        w2b = sbB("w2b", [128, 32]); xddb = sbB("xddb", [128, 2048], BF16)
        NL = 4
        Lq = [sbB(f"Lq{i}", [128, 512], BF16) for i in range(NL)]
        Mq = [sbB(f"Mq{i}", [128, 4, 128], BF16) for i in range(NL)]
        toff = [sbB(f"toff{i}", [128, 512]) for i in range(2)]
        Ysb2 = [sbB(f"Ysb{i}", [128, 2048]) for i in range(2)]; ssg = sbB("ssg", [128, 4]); rs4 = sbB("rs4", [128, 4]); junkb = sbB("junkb", [128, 512])
        ynb = sbB("ynb", [128, 2048], BF16); yT = sbB("yT", [128, 16, 128], BF16)
        ARG = [2, 3, 4]
        rr = {"s": 0, "r": 0, "y": 0, "l": 0}

        def sbank():
            return 5

        def rbank():
            rr["r"] += 1
            return 6 + rr["r"] % 2

        def load_r(c):
            i = c % 2
            I = rinp[i]
            rows = slice(c * 128, (c + 1) * 128)
            DMA(I["qT"][:], QT[:, :, rows].rearrange("h d t -> d h t"), [f"QT{h}_{c // 4}" for h in range(4)], [f"bqT{i}"])
            DMA(I["kT"][:], KT[:, :, rows].rearrange("h d t -> d h t"), [f"KT{h}_{c // 4}" for h in range(4)], [f"bkT{i}"])
            DMA(I["k"][:], K_tm[rows, :], ["K_tm"], [f"bk{i}"])
            DMA(I["v"][:], V_tm[rows, :], ["V_tm"], [f"bv{i}"])
            DMA(I["g"][:], G_tm[rows, :], ["G_tm"], [f"bg{i}"])
            DMA(I["rf"][:], RFs[c], [f"RFs{c}"], [f"brf{i}"])

        def load_s(c):
            i = c % 2
            I = sinp[i]
            rows = slice(c * 128, (c + 1) * 128)
            DMA(I["dt"][:], DT_tm[rows, :], ["DT_tm"], [f"bdt{i}"])
            DMA(I["xs"][:], XS_tm[rows, :], ["XS_tm"], [f"bxs{i}"])
            DMA(I["bT"][:], BT[:, :, rows].rearrange("h d t -> d h t"), [f"BT{h}_{c // 3}" for h in range(4)], [f"bbT{i}"])
            DMA(I["cT"][:], CT[:, :, rows].rearrange("h d t -> d h t"), [f"CT{h}_{c // 3}" for h in range(4)], [f"bcT{i}"])
            DMA(I["b"][:], B_tm[rows, :], ["B_tm"], [f"bb{i}"])
            DMA(zin[c % 3][:], Z_tm[rows, :], ["Z_tm"], [f"bz{c % 3}"])
            DMA(I["pf"][:], PFs[c], [f"PFs{c}"], [f"bpf{i}"])

        def emit_R(c):
            i = c % 2
            I = rinp[i]
            n_ = lambda s: f"b{s}{i}"
            bS = rbank()
            MM([(bk(bS)[:, h * 128:(h + 1) * 128], I["kT"][:, h, :], I["qT"][:, h, :], True, True) for h in range(4)],
               [n_("kT"), n_("qT")], [f"bank{bS}"])
            TT("dve", SD[:], v3(bk(bS), 4), dmT[:], ALU.mult, [f"bank{bS}", "dmT"], ["SD"])
            TT("pool", qf[:], I["qT"][:], qdf[:], ALU.mult, [n_("qT"), "qdf"], ["qf"])
            TT("pool", qb[:], I["qT"][:], qdb[:], ALU.mult, [n_("qT"), "qdb"], ["qb"])
            for hp in range(2):
                b = rbank()
                lst = []
                for hh in range(2):
                    h = 2 * hp + hh
                    o = bk(b)[:, hh * 256:(hh + 1) * 256]
                    lst += [(o, SD[:, h, :], I["v"][:, h * 256:(h + 1) * 256], True, False),
                            (o, qf[:, h, :], I["rf"][:, h * 256:(h + 1) * 256], False, False),
                            (o, qb[:, h, :], Rb_bf[:, h * 256:(h + 1) * 256], False, True)]
                MM(lst, ["SD", "qf", "qb", n_("v"), n_("rf"), "Rb_bf"], [f"bank{b}"])
                CP("act", yr[:, hp * 512:(hp + 1) * 512], bk(b), [f"bank{b}"], ["yr"])
            TT("dve", v3(kdb[:], 4), v3(I["k"][:], 4), bc(kdec[:, 4:8], 128), ALU.mult, [n_("k"), "kdec"], ["kdb"])
            for hp in range(2):
                b = rbank()
                MM([(bk(b)[:, hh * 256:(hh + 1) * 256], kdb[:, (2 * hp + hh) * 128:(2 * hp + hh + 1) * 128],
                     I["v"][:, (2 * hp + hh) * 256:(2 * hp + hh + 1) * 256], True, True) for hh in range(2)],
                   ["kdb", n_("v")], [f"bank{b}"])
                for hh in range(2):
                    h = 2 * hp + hh
                    STT(Rb[:, h * 256:(h + 1) * 256], Rb[:, h * 256:(h + 1) * 256], CD[h], bk(b)[:, hh * 256:(hh + 1) * 256],
                        ALU.mult, ALU.add, ["Rb", f"bank{b}"], ["Rb"])
            CP("act", Rb_bf[:], Rb[:], ["Rb"], ["Rb_bf"])
            for h in range(4):
                P.op("dve", lambda e, h=h: e.bn_stats(out=st6[:, h, :], in_=yr[:, h * 256:(h + 1) * 256]), ["yr"], ["st6"])
            for h in range(4):
                P.op("dve", lambda e, h=h: e.bn_aggr(out=mv[:, h, :], in_=st6[:, h, :]), ["st6"], ["mv"])
            ACTV(lnt[:], mv[:, :, 1], AF.Ln, ["mv"], ["lnt"], bias=EPS)
            ACTV(rstd[:], lnt[:], AF.Exp, ["lnt"], ["rstd"], scale=-0.5)
            for h in range(4):
                TS("dve", yr[:, h * 256:(h + 1) * 256], yr[:, h * 256:(h + 1) * 256], mv[:, h, 0:1], rstd[:, h:h + 1],
                   ALU.subtract, ALU.mult, ["yr", "mv", "rstd"], ["yr"])
            TT("dve", yg[:], yr[:], I["g"][:], ALU.mult, ["yr", n_("g")], ["yg"])
            b = rbank()
            TR([(bkbf(b)[:, k, :], yg[:, k * 128:(k + 1) * 128]) for k in range(8)], ["yg"], [f"bank{b}"])
            CP("act", ygT[:], bkbf(b), [f"bank{b}"], ["ygT"])
            DMA(YRT[c], ygT[:], ["ygT"], [f"YRT{c}"])

        def emit_pro(c):
            i = c % 2
            I = sinp[i]
            n_ = lambda s: f"b{s}{i}"
            TT("dve", dta2[:], I["dt"][:], a_bc[:], ALU.mult, [n_("dt"), "a_bc"], ["dta2"])
            bA = rbank()
            MM([(bk(bA)[:, 0:32], tri[:, 0, :], dta2[:, 0:32], True, True),
                (bk(bA)[:, 32:64], tri[:, 1, :], dta2[:, 32:64], True, True),
                (bk(bA)[:, 64:96], tri[:, 4, :], dta2[:, 0:32], True, True),
                (bk(bA)[:, 96:128], tri[:, 4, :], dta2[:, 32:64], True, True),
                (bk(bA)[:, 128:160], tri[:, 3, :], dta2[:, 32:64], True, True)], ["tri", "dta2"], [f"bank{bA}"])
            CP("act", acs[i][:], bk(bA)[:, 0:64], [f"bank{bA}"], [f"acs{i}"])
            ACTV(E[i][:], bk(bA)[:, 0:160], AF.Exp, [f"bank{bA}"], [f"E{i}"])
            acv = v3(acs[i][:], 2)
            CP("dve", A3[:, :, 0:32], acv, [f"acs{i}"], ["A3"])
            TT("dve", r1[:], acv, A3[:, :, 0:32], ALU.subtract, [f"acs{i}", "A3"], ["r1"])
            CP("dve", A3[:, :, 32:64], r1[:], ["r1"], ["A3"])
            TT("dve", r2[:], r1[:], A3[:, :, 32:64], ALU.subtract, ["r1", "A3"], ["r2"])
            CP("dve", A3[:, :, 64:96], r2[:], ["r2"], ["A3"])
            b = rbank()
            TR([(bkbf(b)[0:96, d, :], A3[:, d, :]) for d in range(2)], ["A3"], [f"bank{b}"])
            CP("act", aT3[i][:], bkbf(b)[0:96, 0:2, :], [f"bank{b}"], [f"aT3_{i}"])
            ACTV(naT3[i][:], bkbf(b)[0:96, 0:2, :], AF.Identity, [f"bank{b}"], [f"naT3_{i}"], scale=-1.0)
            bC = rbank()
            MM([(bk(bC)[:, g * 128:(g + 1) * 128], I["bT"][:, g, :], I["cT"][:, g, :], True, True) for g in range(4)],
               [n_("bT"), n_("cT")], [f"bank{bC}"])
            for d in range(2):
                TT("dve", v3(cbm[i][:, d, :], 4), v3(bk(bC), 4), mask[:, d:d + 1, :].to_broadcast([128, 4, 128]), ALU.mult,
                   [f"bank{bC}", "mask"], [f"cbm{i}"])
            TT("pool", v3(xsd[i][:], 32), v3(I["xs"][:], 32), bc(dsk_bc[:], 64), ALU.mult, [n_("xs"), "dsk_bc"], [f"xsd{i}"])
            for d in range(2):
                TT("pool", v3(xd[i][d][:], 32), v3(I["xs"][:], 32), bc(I["dt"][:, d * 32:(d + 1) * 32], 64), ALU.mult,
                   [n_("xs"), n_("dt")], [f"xd{i}_{d}"])

        def emit_main(c):
            i = c % 2
            I = sinp[i]
            n_ = lambda s: f"b{s}{i}"
            Ysb = Ysb2[i]
            nYsb = f"Ysb{i}"
            iters = [(g, d, bq) for g in range(4) for d in range(2) for bq in (2 * g, 2 * g + 1)]
            ybank = {}

            def A(k):
                g, d, bq = iters[k]
                bL = ARG[k % 3]
                lst = [(bk(bL), ident[:], negm[:, d, :], True, False),
                       (bk(bL), naT3[i][:, d, :], sel[:, bq * 512:(bq + 1) * 512], False, False)]
                for hh in range(4):
                    h = 4 * bq + hh
                    lst.append((bk(bL)[:, hh * 128:(hh + 1) * 128], sel[:, h * 128:(h + 1) * 128], aT3[i][:, d, :], False, hh == 3))
                MM(lst, ["ident", "negm", f"naT3_{i}", f"aT3_{i}", "sel"], [f"bank{bL}"])

            slot = {}

            def EM(k):
                g, d, bq = iters[k]
                bL = ARG[k % 3]
                li = rr["l"] % NL
                rr["l"] += 1
                slot[k] = li
                ACTV(Lq[li][:], bk(bL), AF.Exp, [f"bank{bL}"], [f"Lq{li}"])
                TT("dve", Mq[li][:], v3(Lq[li][:], 4),
                   cbm[i][:, d, g * 128:(g + 1) * 128].unsqueeze(1).to_broadcast([128, 4, 128]), ALU.mult,
                   [f"Lq{li}", f"cbm{i}"], [f"Mq{li}"])

            def YM(k):
                g, d, bq = iters[k]
                li = slot[k]
                lst = []
                yb = ybank[g]
                for hh in range(4):
                    h = 4 * bq + hh
                    last = (k % 4 == 3 and hh == 3)
                    lst.append((bk(yb)[:, (h % 8) * 64:(h % 8 + 1) * 64], Mq[li][:, hh, :], xd[i][d][:, h * 64:(h + 1) * 64], False, last))
                MM(lst, [f"Mq{li}", f"xd{i}_{d}"], [f"bank{yb}"])

            def group_tail(g):
                yb = ybank[g]
                for d2 in range(2):
                    bO = sbank()
                    prev = I["pf"] if d2 == 0 else Sb_bf
                    MM([(bk(bO), I["cT"][:, g, :], prev[:, g * 512:(g + 1) * 512], True, True)],
                       [n_("cT"), n_("pf") if d2 == 0 else "Sb_bf"], [f"bank{bO}"])
                    TT("dve", v3(toff[d2][:], 8), v3(bk(bO), 8), bc(E[i][:, d2 * 32 + g * 8:d2 * 32 + (g + 1) * 8], 64), ALU.mult,
                       [f"bank{bO}", f"E{i}"], [f"toff{d2}"])
                yg_ = Ysb[:, g * 512:(g + 1) * 512]
                TT("dve", yg_, bk(yb), toff[0][:], ALU.add, [f"bank{yb}", "toff0"], [nYsb])
                TT("pool", yg_, yg_, toff[1][:], ALU.add, [nYsb, "toff1"], [nYsb])

            A(0)
            A(1)
            for k in range(17):
                if k < 16:
                    g, d, bq = iters[k]
                    if k % 4 == 0:
                        yb = rr["y"] % 2
                        rr["y"] += 1
                        ybank[g] = yb
                        MM([(bk(yb), ident[:], xsd[i][:, g * 512:(g + 1) * 512], True, False)], ["ident", f"xsd{i}"], [f"bank{yb}"])
                    if k + 2 < 16:
                        A(k + 2)
                    EM(k)
                if k >= 1:
                    YM(k - 1)
                    if (k - 1) % 4 == 3:
                        group_tail(iters[k - 1][0])
                if False:
                    yb = ybank[g]
                    for d2 in range(2):
                        bO = sbank()
                        prev = I["pf"] if d2 == 0 else Sb_bf
                        MM([(bk(bO), I["cT"][:, g, :], prev[:, g * 512:(g + 1) * 512], True, True)],
                           [n_("cT"), n_("pf") if d2 == 0 else "Sb_bf"], [f"bank{bO}"])
                        TT("dve", v3(toff[d2][:], 8), v3(bk(bO), 8), bc(E[i][:, d2 * 32 + g * 8:d2 * 32 + (g + 1) * 8], 64), ALU.mult,
                           [f"bank{bO}", f"E{i}"], [f"toff{d2}"])
                    yg_ = Ysb[:, g * 512:(g + 1) * 512]
                    TT("dve", yg_, bk(yb), toff[0][:], ALU.add, [f"bank{yb}", "toff0"], ["Ysb"])
                    TT("pool", yg_, yg_, toff[1][:], ALU.add, ["Ysb", "toff1"], ["Ysb"])
                    if P.cap is not None:
                        P.cap.append(("mark",))
            TT("dve", w2b[:], I["dt"][:, 32:64], E[i][:, 128:160], ALU.mult, [n_("dt"), f"E{i}"], ["w2b"])
            TT("pool", v3(xddb[:], 32), v3(I["xs"][:], 32), bc(w2b[:], 64), ALU.mult, [n_("xs"), "w2b"], ["xddb"])
            TT("pool", v3(Sb[:], 32), v3(Sb[:], 32), bc(E[i][:, 96:128], 64), ALU.mult, ["Sb", f"E{i}"], ["Sb"])
            for g in range(4):
                b = sbank()
                MM([(bk(b), I["b"][:, g * 128:(g + 1) * 128], xddb[:, g * 512:(g + 1) * 512], True, True)],
                   [n_("b"), "xddb"], [f"bank{b}"])
                sg_ = Sb[:, g * 512:(g + 1) * 512]
                TT("dve", sg_, sg_, bk(b), ALU.add, ["Sb", f"bank{b}"], ["Sb"])
            CP("act", Sb_bf[:], Sb[:], ["Sb"], ["Sb_bf"])
        def emit_tailB(c, bankfn):
            i = c % 2
            Ysb = Ysb2[i]
            nYsb = f"Ysb{i}"
            zt, nz = zin[c % 3], f"bz{c % 3}"
            TT("pool", Ysb[:], Ysb[:], zt[:], ALU.mult, [nYsb, nz], [nYsb])
            for g in range(4):
                ACTV(junkb[:], Ysb[:, g * 512:(g + 1) * 512], AF.Square, [nYsb], ["junkb", "ssg"], accum=ssg[:, g:g + 1])
            ACTV(lnt2[:], ssg[:], AF.Ln, ["ssg"], ["lnt2"], scale=1.0 / 512, bias=EPS)
            ACTV(rs4[:], lnt2[:], AF.Exp, ["lnt2"], ["rs4"], scale=-0.5)
            for g in range(4):
                ACTV(ynb[:, g * 512:(g + 1) * 512], Ysb[:, g * 512:(g + 1) * 512], AF.Identity, [nYsb, "rs4"], ["ynb"],
                     scale=rs4[:, g:g + 1])
            for hf in range(2):
                b = bankfn()
                TR([(bkbf(b)[:, k, :], ynb[:, (hf * 8 + k) * 128:(hf * 8 + k + 1) * 128]) for k in range(8)], ["ynb"], [f"bank{b}"])
                CP("act", yT[:, hf * 8:(hf + 1) * 8, :], bkbf(b), [f"bank{b}"], ["yT"])
            DMA(YST[c], yT[:], ["yT"], [f"YST{c}"])

        load_r(NCH - 1)
        load_s(NCH - 1)
        emit_pro(NCH - 1)
        for c in range(NCH - 1, -1, -1):
            if c > 0:
                load_s(c - 1)
                load_r(c - 1)
            P.begin_capture(); emit_main(c); emit_tailB(c, sbank); Lm = P.end_capture()
            Lm2 = [it for it in Lm if it[0] != "mark"]
            P.begin_capture()
            emit_R(c)
            if c > 0:
                emit_pro(c - 1)
            L2 = P.end_capture()
            P.replay(merge_threads([Lm2, L2], [0.0, 0.02], [1.0, 0.95]))
        P.barrier()
        sB.close()
        sFB.close()
        if dbg == "pb":
            DMA(out[0:128, :], x_in[0:128, :], [], ["out"])
            P.emit(nc)
            return nc

        sC = ExitStack()
        sbC = mk(sC)
        u2T = sbC("u2T", [128, 8, LP + 2], BF16)
        MSET("pool", u2T[:, :, 0:1], 0.0, ["u2T_h0"])
        MSET("pool", u2T[:, :, LP + 1:LP + 2], 0.0, ["u2T_h1"])
        sC1 = ExitStack()
        sbC1 = mk(sC1)
        wro = sbC1("wro", [128, 8, 1024], BF16); wso = sbC1("wso", [128, 16, 1024], BF16); wo = sbC1("wo", [128, 8, 1024], BF16)
        ncol = sbC1("ncol", [128, 24])
        DMA(ncol[:], p_ncol, [], ["ncol"])
        ce = [0]
        for (wt, wn, src, nk, c0) in [(wro, "wro", w_ret_out, 8, 0), (wso, "wso", w_ssd_out, 16, 8), (wo, "wo", w_out, 8, None)]:
            for k2 in range(nk // 2):
                hf = ce[0] % 2
                sv = v3(stg[:, hf * 2048:(hf + 1) * 2048], 2)
                DMA(sv, src[k2 * 256:(k2 + 1) * 256, :].rearrange("(k p) n -> p k n", p=128), [], [f"cstage_h{hf}"])
                for kk in range(2):
                    k = k2 * 2 + kk
                    eng = ["act", "dve", "pool"][ce[0] % 3]
                    ce[0] += 1
                    if c0 is None:
                        CP(eng, wt[:, k, :], sv[:, kk, :], [f"cstage_h{hf}"], [wn])
                    elif eng == "act":
                        ACTV(wt[:, k, :], sv[:, kk, :], AF.Identity, [f"cstage_h{hf}", "ncol"], [wn], scale=ncol[:, c0 + k:c0 + k + 1])
                    else:
                        TS(eng, wt[:, k, :], sv[:, kk, :], ncol[:, c0 + k:c0 + k + 1], None, ALU.mult, None, [f"cstage_h{hf}", "ncol"], [wn])
                ce[0] += 1
        nfw_bc = sbC1("nfw_bc", [128, D]); DMA(nfw_bc[:], r_nfw.partition_broadcast(128), [], ["nfw_bc"])
        cinp = [dict(yr=sbC1(f"cyr{i}", [128, 8, 128], BF16), ys=sbC1(f"cys{i}", [128, 16, 128], BF16),
                     gg=sbC1(f"cgg{i}", [128, 2048], BF16)) for i in range(2)]
        m1 = sbC1("m1", [128, 512]); m2 = sbC1("m2", [128, 512])
        mg = [sbC1(f"mg{i}", [128, 1024], BF16) for i in range(2)]
        mT = sbC1("mT", [128, 8, 128], BF16)
        h2 = [sbC1(f"h2_{i}", [128, D]) for i in range(3)]
        junkc = stg[:, 0:D]
        u2 = [sbC1(f"u2_{i}", [128, D], BF16) for i in range(2)]
        ssc = [sbC1(f"ssc{i}", [128, 1]) for i in range(2)]
        rsc = [sbC1(f"rsc{i}", [128, 1]) for i in range(2)]

        def pc1_load(c):
            i = c % 2
            I = cinp[i]
            hi = c % 3
            rows = slice(c * 128, (c + 1) * 128)
            DMA(I["yr"][:], YRT[c], [f"YRT{c}"], [f"cyr{i}"])
            DMA(I["ys"][:], YST[c], [f"YST{c}"], [f"cys{i}"])
            DMA(I["gg"][:], GG_tm[rows, :], ["GG_tm"], [f"cgg{i}"])
            if c == 0:
                MSET("pool", h2[hi][:], 0.0, [f"h2_{hi}"])
                DMA(h2[hi][PAD:128, :], meta_in, [], [f"h2_{hi}"])
            else:
                DMA(h2[hi][:], x_in[(c - 1) * 128:c * 128, :], [], [f"h2_{hi}"])

        def pc1_A(c):
            i = c % 2
            I = cinp[i]
            for hf in range(2):
                cs = slice(hf * 512, (hf + 1) * 512)
                bR, bS_ = getbank(), getbank()
                MM([(bk(bR), I["yr"][:, k, :], wro[:, k, cs], k == 0, k == 7) for k in range(8)], [f"cyr{i}", "wro"], [f"bank{bR}"])
                MM([(bk(bS_), I["ys"][:, k, :], wso[:, k, cs], k == 0, k == 15) for k in range(16)], [f"cys{i}", "wso"], [f"bank{bS_}"])
                TT("dve", m1[:], bk(bR), I["gg"][:, hf * 512:(hf + 1) * 512], ALU.mult, [f"bank{bR}", f"cgg{i}"], ["m1"])
                TT("dve", m2[:], bk(bS_), I["gg"][:, 1024 + hf * 512:1024 + (hf + 1) * 512], ALU.mult, [f"bank{bS_}", f"cgg{i}"], ["m2"])
                TT("pool", mg[i][:, cs], m1[:], m2[:], ALU.add, ["m1", "m2"], [f"mg{i}"])

        def pc1_B(c):
            i = c % 2
            hi = c % 3
            rows = slice(c * 128, (c + 1) * 128)
            b = getbank()
            TR([(bkbf(b)[:, k, :], mg[i][:, k * 128:(k + 1) * 128]) for k in range(8)], [f"mg{i}"], [f"bank{b}"])
            CP("act", mT[:], bkbf(b), [f"bank{b}"], ["mT"])
            for hf in range(2):
                cs = slice(hf * 512, (hf + 1) * 512)
                b = getbank()
                MM([(bk(b), mT[:, k, :], wo[:, k, cs], k == 0, k == 7) for k in range(8)], ["mT", "wo"], [f"bank{b}"])
                TT("dve", h2[hi][:, cs], bk(b), h2[hi][:, cs], ALU.add, [f"bank{b}", f"h2_{hi}"], [f"h2_{hi}"])
            DMA(H2[rows, :], h2[hi][:], [f"h2_{hi}"], [f"H2_{c}"])
            ACTV(junkc, h2[hi][:], AF.Square, [f"h2_{hi}"], ["cstage", "cstage_h0", f"ssc{i}"], accum=ssc[i][:])
            ACTV(rsc[i][:], ssc[i][:], AF.Sqrt, [f"ssc{i}"], [f"rsc{i}"], scale=1.0 / D, bias=EPS)
            P.op("dve", lambda e: e.reciprocal(out=rsc[i][:], in_=rsc[i][:]), [f"rsc{i}"], [f"rsc{i}"])
            STT(u2[i][:], h2[hi][:], rsc[i][:], nfw_bc[:], ALU.mult, ALU.mult, [f"h2_{hi}", f"rsc{i}", "nfw_bc"], [f"u2_{i}"])

        def pc1_C(c):
            i = c % 2
            b = getbank()
            TR([(bkbf(b)[:, k, :], u2[i][:, k * 128:(k + 1) * 128]) for k in range(8)], [f"u2_{i}"], [f"bank{b}"])
            CP("act", u2T[:, :, 1 + c * 128:1 + (c + 1) * 128], bkbf(b), [f"bank{b}"], [f"u2T_{c}"])

        pc1_load(0)
        for t in range(NCH + 2):
            if t + 1 < NCH:
                pc1_load(t + 1)
            if t < NCH:
                pc1_A(t)
            if 0 <= t - 1 < NCH:
                pc1_B(t - 1)
            if 0 <= t - 2 < NCH:
                pc1_C(t - 2)
        u2T_all = [f"u2T_{c}" for c in range(NCH)] + ["u2T_h0", "u2T_h1"]
        P.barrier()
        sC1.close()
        if dbg == "pc1":
            DMA(out[0:128, :], x_in[0:128, :], [], ["out"])
            P.emit(nc)
            sC.close()
            return nc

        sD = ExitStack()
        sbD = mk(sD)
        wdn = sbD("wdn", [128, 22, 1024], BF16)
        stgs2 = [(stg, "cstage"), (stg, "cstage")]
        for j4 in range(6):
            nj = 4 if j4 < 5 else 2
            s, sn = stgs2[j4 % 2]
            DMA(v3(s[:, 0:nj * 1024], nj), w_down[j4 * 512:j4 * 512 + nj * 128, :].rearrange("(k p) n -> p k n", p=128), [], [sn])
            for kk in range(nj):
                CP(["act", "dve", "pool"][(j4 * 4 + kk) % 3], wdn[:, j4 * 4 + kk, :], v3(s[:, 0:nj * 1024], nj)[:, kk, :], [sn], ["wdn"])
        fcw = sbD("fcw", [128, 132]); fcb = sbD("fcb", [128, 44])
        DMA(fcw[:], p_fcw, [], ["fcw"]); DMA(fcb[:], p_fcb, [], ["fcb"])
        fnw_bc = sbD("fnw_bc", [128, D]); DMA(fnw_bc[:], r_fnw.partition_broadcast(128), [], ["fnw_bc"])
        actT = sbD("actT", [128, 22, 512], BF16)
        wp = [sbD(f"wp{i}", [128, 8, 256], BF16) for i in range(3)]
        cg = [sbD(f"cg{i}", [128, 256]) for i in range(2)]
        cu = [sbD(f"cu{i}", [128, 256]) for i in range(2)]
        sgt = [sbD(f"sgt{i}", [128, 256]) for i in range(2)]
        h3 = [sbD(f"h3_{i}", [128, D]) for i in range(2)]
        ot2 = [sbD(f"ot2_{i}", [128, D]) for i in range(2)]
        junkd = stg[:, 0:D]
        ssd_ = [sbD(f"ssd{i}", [128, 1]) for i in range(2)]
        rsd = [sbD(f"rsd{i}", [128, 1]) for i in range(2)]
        WUPp = WUPb.rearrange("p k (j c) -> p k j c", j=22)
        pc = [0]
        cc = [0]
        tcn = [0]
        for bI in range(8):
            T0 = 128 + bI * 512
            for j in range(22):
                wi = pc[0] % 3
                pc[0] += 1
                DMA(wp[wi][:], WUPp[:, :, j, :], ["WUPb0", "WUPb6", "WUPb12", "WUPb17"], [f"wp{wi}"])
                for sub in range(2):
                    ts = T0 + sub * 256
                    ci = cc[0] % 2
                    cc[0] += 1
                    bg, bu = getbank(), getbank()
                    MM([(bk(bg)[:, 0:258], wp[wi][:, k, 0:128], u2T[:, k, ts:ts + 258], k == 0, k == 7) for k in range(8)],
                       [f"wp{wi}"] + u2T_all, [f"bank{bg}"])
                    MM([(bk(bu)[:, 0:258], wp[wi][:, k, 128:256], u2T[:, k, ts:ts + 258], k == 0, k == 7) for k in range(8)],
                       [f"wp{wi}"] + u2T_all, [f"bank{bu}"])
                    for (bb2, T, tn, m) in [(bg, cg[ci], f"cg{ci}", j), (bu, cu[ci], f"cu{ci}", 22 + j)]:
                        ACTV(T[:], bk(bb2)[:, 1:257], AF.Identity, [f"bank{bb2}", "fcw", "fcb"], [tn],
                             scale=fcw[:, m * 3 + 1:m * 3 + 2], bias=fcb[:, m:m + 1])
                        STT(T[:], bk(bb2)[:, 0:256], fcw[:, m * 3:m * 3 + 1], T[:], ALU.mult, ALU.add, [f"bank{bb2}", tn], [tn])
                        STT(T[:], bk(bb2)[:, 2:258], fcw[:, m * 3 + 2:m * 3 + 3], T[:], ALU.mult, ALU.add, [f"bank{bb2}", tn], [tn])
                    ACTV(sgt[ci][:], cg[ci][:], AF.Silu, [f"cg{ci}"], [f"sgt{ci}"])
                    TT("pool", actT[:, j, sub * 256:(sub + 1) * 256], sgt[ci][:], cu[ci][:], ALU.mult, [f"sgt{ci}", f"cu{ci}"], ["actT"])
            for ti in range(4):
                tok0 = T0 + ti * 128
                i = tcn[0] % 2
                tcn[0] += 1
                DMA(h3[i][:], H2[tok0:tok0 + 128, :], [f"H2_{tok0 // 128}"], [f"h3_{i}"])
                for hf in range(2):
                    cs = slice(hf * 512, (hf + 1) * 512)
                    b = getbank()
                    MM([(bk(b), actT[:, j, ti * 128:(ti + 1) * 128], wdn[:, j, cs], j == 0, j == 21) for j in range(22)],
                       ["actT", "wdn"], [f"bank{b}"])
                    TT("dve", h3[i][:, cs], bk(b), h3[i][:, cs], ALU.add, [f"bank{b}", f"h3_{i}"], [f"h3_{i}"])
                ACTV(junkd, h3[i][:], AF.Square, [f"h3_{i}"], ["cstage", f"ssd{i}"], accum=ssd_[i][:])
                ACTV(rsd[i][:], ssd_[i][:], AF.Sqrt, [f"ssd{i}"], [f"rsd{i}"], scale=1.0 / D, bias=EPS)
                P.op("dve", lambda e, i=i: e.reciprocal(out=rsd[i][:], in_=rsd[i][:]), [f"rsd{i}"], [f"rsd{i}"])
                STT(ot2[i][:], h3[i][:], rsd[i][:], fnw_bc[:], ALU.mult, ALU.mult, [f"h3_{i}", f"rsd{i}", "fnw_bc"], [f"ot2_{i}"])
                DMA(out[tok0 - 128:tok0, :], ot2[i][:], [f"ot2_{i}"], [f"out{tok0}"])
        sD.close()
        sC.close()
        P.emit(nc)
    return nc


def _host_inputs(inputs, b):
    f = lambda a: np.ascontiguousarray(np.asarray(a, dtype=np.float32))
    m = {
        "x": f(inputs["x"][b]),
        "meta": f(inputs["meta_tokens"]),
        "w_in": f(inputs["w_in"][0]),
        "w_ret_out": f(inputs["w_ret_out"][0]),
        "w_ssd_out": f(inputs["w_ssd_out"][0]),
        "w_out": f(inputs["w_out"][0]),
        "w_ffn_up": f(inputs["w_ffn_up"][0]),
        "w_ffn_down": f(inputs["w_ffn_down"][0]),
        "r_nmw": f(inputs["norm_mix_w"][0]), "r_nfw": f(inputs["norm_ffn_w"][0]), "r_fnw": f(inputs["final_norm_w"]),
        "r_dtb": f(np.concatenate([inputs["dt_bias_f"][0], inputs["dt_bias_b"][0]])),
        "r_alog": f(np.concatenate([inputs["a_log_f"][0], inputs["a_log_b"][0]])),
        "r_dskip": f(inputs["d_skip"][0]),
        "p_scw": f(np.asarray(inputs["w_ssd_conv"][0]).reshape(3, 24, 128).transpose(2, 1, 0).reshape(128, 72)),
        "p_scb": f(np.asarray(inputs["b_ssd_conv"][0]).reshape(24, 128).T),
        "p_fcw": f(np.asarray(inputs["w_ffn_conv"][0]).reshape(3, 44, 128).transpose(2, 1, 0).reshape(128, 132)),
        "p_fcb": f(np.asarray(inputs["b_ffn_conv"][0]).reshape(44, 128).T),
        "p_ncol": f(np.concatenate([np.asarray(inputs["ret_gn_w"][0]).reshape(8, 128).T,
                                    np.asarray(inputs["ssd_norm_w"][0]).reshape(16, 128).T], axis=1)),
    }
    return m


def kernel(**inputs):
    nc = build()
    cst = _consts()
    in_maps = []
    for b in range(8):
        m = _host_inputs(inputs, b)
        m.update(cst)
        in_maps.append(m)
    res = run_bass_kernel_spmd(nc, in_maps, core_ids=list(range(8)))
    return np.stack([np.asarray(r["out"], dtype=np.float32) for r in res.results], 0)
```

```python
import numpy as np
from contextlib import ExitStack
import concourse.bass as bass
import concourse.mybir as mybir
from concourse.bass_utils import run_bass_kernel_spmd

AF = mybir.ActivationFunctionType
ALU = mybir.AluOpType
F32 = mybir.dt.float32
BF16 = mybir.dt.bfloat16
AX = mybir.AxisListType

D = 1024
SEQ = 4096
NMETA = 16
CH = 128
PAD = CH - NMETA
LP = PAD + NMETA + SEQ
NCH = LP // CH
RH = 4
SH = 32
SG = 4
DFF = 2816
EPS = 1e-6
DIN = 10304
OQ, OK_, OV, OG, OZ, OX, ODF, ODB, OGR, OGS = 0, 512, 1024, 2048, 3072, 5120, 8192, 8224, 8256, 9280


class Buf:
    __slots__ = ("name", "lw", "rd")

    def __init__(self, name):
        self.name = name
        self.lw = None
        self.rd = []


class Prog:
    ENG = ["pe", "act", "dve", "pool", "sp"]
    DMAQ = {"sp": 12, "pool": 6, "act": 4}

    def __init__(self):
        self.ops = {e: [] for e in self.ENG}
        self.n = {e: 0 for e in self.ENG}
        self.nd = {q: 0 for q in self.DMAQ}
        self.seen = {e: {} for e in self.ENG}
        self.bufs = {}
        self.pending = {e: {} for e in self.ENG}
        self.cap = None

    def begin_capture(self):
        self.cap = []

    def end_capture(self):
        l, self.cap = self.cap, None
        return l

    def replay(self, lst):
        for (kind, a, fn, r, w) in lst:
            if kind == "op":
                self.op(a, fn, r, w)
            else:
                self.dma(a, fn, r, w)

    def barrier(self):
        snap = {}
        for e in ["pe", "act", "dve", "pool"]:
            if self.n[e] > 0:
                snap[e] = self.n[e]
        for q, ns in self.DMAQ.items():
            i = self.nd[q]
            for slot in range(min(ns, i)):
                snap[("dma", q, slot)] = 16 * ((i - 1 - slot) // ns + 1)
        for e in self.ENG:
            d = self.pending[e]
            for k, v in snap.items():
                if k == e and e in ("pe", "act", "dve", "pool"):
                    continue
                if v > d.get(k, 0):
                    d[k] = v

    def buf(self, name):
        b = self.bufs.get(name)
        if b is None:
            b = self.bufs[name] = Buf(name)
        return b

    def _deps(self, eng, r, w, dma):
        waits = dict(self.pending[eng])
        self.pending[eng] = {}

        def need(dep, war=False):
            if dep is None:
                return
            key, val = dep
            if key == eng and not dma:
                if eng == "pe" or war:
                    return
            if val > waits.get(key, 0):
                waits[key] = val
        for b in r:
            need(b.lw)
        for b in w:
            need(b.lw)
            for d in b.rd:
                need(d, True)
        out = []
        for k, v in waits.items():
            if v > self.seen[eng].get(k, 0):
                self.seen[eng][k] = v
                out.append((k, v))
        return out

    def _mark(self, tok, r, w):
        for b in r:
            b.rd.append(tok)
        for b in w:
            b.lw = tok
            b.rd = []

    def op(self, eng, fn, r=(), w=()):
        if self.cap is not None:
            self.cap.append(("op", eng, fn, list(r), list(w)))
            return
        r = [self.buf(x) if isinstance(x, str) else x for x in r]
        w = [self.buf(x) if isinstance(x, str) else x for x in w]
        waits = self._deps(eng, r, w, False)
        self.n[eng] += 1
        tok = (eng, self.n[eng])
        self._mark(tok, r, w)
        self.ops[eng].append((waits, fn, (eng, 1)))

    def dma(self, q, fn, r=(), w=()):
        if self.cap is not None:
            self.cap.append(("dma", q, fn, list(r), list(w)))
            return
        r = [self.buf(x) if isinstance(x, str) else x for x in r]
        w = [self.buf(x) if isinstance(x, str) else x for x in w]
        waits = self._deps(q, r, w, True)
        i = self.nd[q]
        self.nd[q] += 1
        ns = self.DMAQ[q]
        slot, rnd = i % ns, i // ns
        key = ("dma", q, slot)
        if rnd > 0 and 16 * rnd > self.seen[q].get(key, 0):
            self.seen[q][key] = 16 * rnd
            waits.append((key, 16 * rnd))
        tok = (key, 16 * (rnd + 1))
        self._mark(tok, r, w)
        self.ops[q].append((waits, fn, (key, 16)))

    def emit(self, nc, final_waits_engine="sp"):
        keys = set()
        for e in self.ENG:
            for waits, fn, (k, inc) in self.ops[e]:
                keys.add(k)
        with ExitStack() as st:
            sems = {}
            for k in sorted(keys, key=str):
                nm = "s_" + ("_".join(str(x) for x in k) if isinstance(k, tuple) else k)
                sems[k] = st.enter_context(nc.semaphore(nm))
            block = st.enter_context(nc.Block())
            finals = {}
            for e in self.ENG:
                for waits, fn, (k, inc) in self.ops[e]:
                    finals[k] = finals.get(k, 0) + inc

            def run(e, engh):
                for waits, fn, (k, inc) in self.ops[e]:
                    for (wk, wv) in waits:
                        engh.wait_ge(sems[wk], wv)
                    ins = fn(engh)
                    ins.then_inc(sems[k], inc)
                if e == final_waits_engine:
                    for k, v in finals.items():
                        engh.wait_ge(sems[k], v)

            block.tensor(lambda eh: run("pe", eh))
            block.scalar(lambda eh: run("act", eh))
            block.vector(lambda eh: run("dve", eh))
            block.gpsimd(lambda eh: run("pool", eh))
            block.sync(lambda eh: run("sp", eh))


def merge_threads(lists, offs, spans):
    items = []
    for i, L in enumerate(lists):
        n = len(L)
        for j, it in enumerate(L):
            items.append((offs[i] + spans[i] * (j + 0.5) / n, i, j, it))
    items.sort(key=lambda t: (t[0], t[1], t[2]))
    return [t[3] for t in items]


GAM = [1.0 - 2.0 ** (-5.0 - h) for h in range(RH)]
NEGBIG = -30000.0


def _consts():
    c = {}
    f32 = np.float32
    c["c_ident"] = np.eye(128, dtype=f32)
    half = 64
    inv = (10000.0 ** (-np.arange(half, dtype=np.float64) / half))
    pos = np.arange(LP, dtype=np.float64) - PAD
    ang = (pos[None, :] * inv[:, None]).astype(f32)
    ang = (pos.astype(f32)[None, :] * inv.astype(f32)[:, None]).astype(f32)
    cos = np.cos(ang).astype(f32)
    sin = np.sin(ang).astype(f32)
    c["c_cos"] = np.concatenate([cos, cos], 0)
    c["c_sin"] = np.concatenate([-sin, sin], 0)
    i = np.arange(128)
    s_, l_ = i[:, None], i[None, :]
    g = np.array(GAM, dtype=np.float64)
    sc = 128.0 ** -0.5
    dm = np.stack([g[h] ** np.abs(l_ - s_) * sc for h in range(RH)], 1)
    c["c_dmT"] = dm.astype(f32)
    c["c_qdf"] = np.stack([np.broadcast_to(g[h] ** (l_ + 1.0), (128, 128)) for h in range(RH)], 1).astype(f32)
    c["c_qdb"] = np.stack([np.broadcast_to(g[h] ** (128.0 - l_), (128, 128)) for h in range(RH)], 1).astype(f32)
    kd = np.zeros((128, 8), f32)
    for h in range(RH):
        kd[:, h] = g[h] ** (127.0 - i) * sc
        kd[:, 4 + h] = g[h] ** (i * 1.0) * sc
    c["c_kdec"] = kd
    tri_f = (s_ <= l_).astype(f32)
    c["c_tri"] = np.stack([tri_f, tri_f.T.copy(), (s_ > l_).astype(f32), (s_ < l_).astype(f32),
                           np.ones((128, 128), f32)], 1)
    c["c_mask"] = np.stack([(l_ >= s_).astype(f32), (l_ <= s_).astype(f32)], 1)
    nf = np.where(l_ < s_, NEGBIG, 0.0).astype(f32)
    nb = np.where(l_ > s_, NEGBIG, 0.0).astype(f32)
    c["c_negm"] = np.stack([np.tile(nf, (1, 4)), np.tile(nb, (1, 4))], 1)
    sel = np.zeros((96, 32, 128), f32)
    for r in range(96):
        sel[r, r % 32, :] = 1.0
    c["c_sel"] = sel.reshape(96, 4096)
    return c


def build(dbg=False):
    nc = bass.Bass("TRN2", target_bir_lowering=False)
    P = Prog()

    def din(name, shape, dt=F32):
        return nc.dram_tensor(name, list(shape), dt, kind="ExternalInput").ap()

    def dscr(name, shape, dt):
        return nc.dram_tensor(name, list(shape), dt, kind=("ExternalOutput" if dbg else "Internal")).ap()

    x_in = din("x", [SEQ, D])
    meta_in = din("meta", [NMETA, D])
    w_in = din("w_in", [D, DIN])
    w_ret_out = din("w_ret_out", [1024, D])
    w_ssd_out = din("w_ssd_out", [2048, D])
    w_out = din("w_out", [D, D])
    w_up = din("w_ffn_up", [D, 2 * DFF])
    w_down = din("w_ffn_down", [DFF, D])
    r_nmw = din("r_nmw", [D]); r_nfw = din("r_nfw", [D]); r_fnw = din("r_fnw", [D])
    r_dtb = din("r_dtb", [64]); r_alog = din("r_alog", [64]); r_dskip = din("r_dskip", [32])
    p_scw = din("p_scw", [128, 72]); p_scb = din("p_scb", [128, 24])
    p_fcw = din("p_fcw", [128, 132]); p_fcb = din("p_fcb", [128, 44])
    p_ncol = din("p_ncol", [128, 24])
    cin = {k: din(k, v.shape) for k, v in _consts().items()}
    out = nc.dram_tensor("out", [SEQ, D], F32, kind="ExternalOutput").ap()

    QT = dscr("QT", [4, 128, LP], BF16); KT = dscr("KT", [4, 128, LP], BF16)
    BT = dscr("BT", [4, 128, LP], BF16); CT = dscr("CT", [4, 128, LP], BF16)
    K_tm = dscr("K_tm", [LP, 512], BF16); V_tm = dscr("V_tm", [LP, 1024], BF16)
    XS_tm = dscr("XS_tm", [LP, 2048], BF16); B_tm = dscr("B_tm", [LP, 512], BF16)
    DT_tm = dscr("DT_tm", [LP, 64], F32)
    Z_tm = dscr("Z_tm", [LP, 2048], BF16); G_tm = dscr("G_tm", [LP, 1024], BF16)
    GG_tm = dscr("GG_tm", [LP, 2048], BF16)
    RFs = dscr("RFs", [NCH, 128, 1024], BF16); PFs = dscr("PFs", [NCH, 128, 2048], BF16)
    YRT = dscr("YRT", [NCH, 128, 8, 128], BF16); YST = dscr("YST", [NCH, 128, 16, 128], BF16)
    H2 = dscr("H2", [LP, D], F32)
    WUPb = dscr("WUPb", [128, 8, 2 * DFF], BF16)

    def TT(eng, o, a, b, op, r, w):
        P.op(eng, lambda e: e.tensor_tensor(out=o, in0=a, in1=b, op=op), r, w)

    def TS(eng, o, a, s1, s2, op0, op1, r, w):
        if op1 is None:
            P.op(eng, lambda e: e.tensor_scalar(out=o, in0=a, scalar1=s1, scalar2=None, op0=op0), r, w)
        else:
            P.op(eng, lambda e: e.tensor_scalar(out=o, in0=a, scalar1=s1, scalar2=s2, op0=op0, op1=op1), r, w)

    def STT(o, a, sc, b, op0, op1, r, w):
        P.op("dve", lambda e: e.scalar_tensor_tensor(out=o, in0=a, scalar=sc, in1=b, op0=op0, op1=op1), r, w)

    def ACTV(o, a, func, r, w, scale=1.0, bias=0.0, accum=None):
        if accum is None:
            P.op("act", lambda e: e.activation(out=o, in_=a, func=func, scale=scale, bias=bias), r, w)
        else:
            P.op("act", lambda e: e.activation(out=o, in_=a, func=func, scale=scale, bias=bias, accum_out=accum), r, w)

    def CP(eng, o, a, r, w):
        if eng == "act":
            P.op("act", lambda e: e.copy(out=o, in_=a), r, w)
        else:
            P.op(eng, lambda e: e.tensor_copy(out=o, in_=a), r, w)

    def MM(lst, r, w):
        def f(e):
            ins = None
            for (o, lt, rh, st_, sp_) in lst:
                ins = e.matmul(o, lt, rh, start=st_, stop=sp_)
            return ins
        P.op("pe", f, r, w)

    def TR(lst, r, w):
        def f(e):
            ins = None
            for (o, a) in lst:
                ins = e.transpose(out=o, in_=a, identity=ident[:])
            return ins
        P.op("pe", f, list(r) + ["ident"], w)

    def DMA(o, a, r, w, q="sp"):
        P.dma(q, lambda e: e.dma_start(out=o, in_=a), r, w)

    def MSET(eng, o, v, w):
        P.op(eng, lambda e: e.memset(o, v), [], w)

    top = ExitStack()
    with top:
        def mk(stack):
            def sb(name, shape, dt=F32):
                return stack.enter_context(nc.sbuf_tensor(name, list(shape), dt))
            return sb
        sbT = mk(top)
        PBK = [top.enter_context(nc.psum_tensor(f"bank{i}", [128, 512], F32)) for i in range(8)]
        bank_rr = [0]

        def getbank(lo=0, hi=8):
            b = lo + bank_rr[0] % (hi - lo)
            bank_rr[0] += 1
            return b

        def bk(b):
            return PBK[b][:]

        def bkbf(b):
            return PBK[b][:].bitcast(BF16).rearrange("p (a b) -> p a b", a=8)

        ident = sbT("ident", [128, 128], BF16)
        stg = sbT("cstage", [128, 4224], F32)
        DMA(stg[:, 0:128], cin["c_ident"], [], ["cstage"])
        CP("dve", ident[:], stg[:, 0:128], ["cstage"], ["ident"])

        def rowbc(name, src, n):
            t = sbT(name, [128, n], F32)
            DMA(t[:], src.partition_broadcast(128), [], [name])
            return t

        s12 = ExitStack()
        sb12 = mk(s12)
        uT = sb12("uT", [128, 8, LP + 2], BF16)
        s1 = ExitStack()
        sb1 = mk(s1)
        wbc = sb1("wbc", [128, D])
        DMA(wbc[:], r_nmw.partition_broadcast(128), [], ["wbc"])
        junk = sb1("junk", [128, D])
        MSET("pool", uT[:, :, 0:1], 0.0, ["uT_h0"])
        MSET("pool", uT[:, :, LP + 1:LP + 2], 0.0, ["uT_h1"])

        def rmsnorm_to_T(X, bx, wtile, bw, dstT, c, bdst, U, bu, SS, bss, RS, brs, junk_t, bjunk, u2name=None):
            if u2name is not None:
                bu = u2name
            ACTV(junk_t if u2name is not None else junk_t[:], X[:], AF.Square, bx, [bjunk, bss], accum=SS[:])
            ACTV(RS[:], SS[:], AF.Sqrt, [bss], [brs], scale=1.0 / D, bias=EPS)
            P.op("dve", lambda e: e.reciprocal(out=RS[:], in_=RS[:]), [brs], [brs])
            STT(U[:], X[:], RS[:], wtile[:], ALU.mult, ALU.mult, list(bx) + [brs, bw], [bu])
            b = getbank()
            TR([(bkbf(b)[:, k, :], U[:, k * 128:(k + 1) * 128]) for k in range(8)], [bu], [f"bank{b}"])
            CP("act", dstT[:, :, 1 + c * 128:1 + (c + 1) * 128], bkbf(b), [f"bank{b}"], [bdst])

        xt = [sb1(f"xt{i}", [128, D]) for i in range(2)]
        ub = [sb1(f"ub{i}", [128, D], BF16) for i in range(2)]
        ssq = [sb1(f"ss{i}", [128, 1]) for i in range(2)]
        rsq = [sb1(f"rs{i}", [128, 1]) for i in range(2)]
        def p1_a(c):
            i = c % 2
            if c == 0:
                MSET("pool", xt[i][:], 0.0, [f"xt{i}"])
                DMA(xt[i][PAD:128, :], meta_in, [], [f"xt{i}"])
            else:
                DMA(xt[i][:], x_in[(c - 1) * 128:c * 128, :], [], [f"xt{i}"])
            X, U, SS, RS = xt[i], ub[i], ssq[i], rsq[i]
            ACTV(junk[:], X[:], AF.Square, [f"xt{i}"], ["junk", f"ss{i}"], accum=SS[:])
            ACTV(RS[:], SS[:], AF.Sqrt, [f"ss{i}"], [f"rs{i}"], scale=1.0 / D, bias=EPS)
            P.op("dve", lambda e: e.reciprocal(out=RS[:], in_=RS[:]), [f"rs{i}"], [f"rs{i}"])
            STT(U[:], X[:], RS[:], wbc[:], ALU.mult, ALU.mult, [f"xt{i}", f"rs{i}", "wbc"], [f"ub{i}"])

        def p1_b(c):
            i = c % 2
            b = getbank()
            TR([(bkbf(b)[:, k, :], ub[i][:, k * 128:(k + 1) * 128]) for k in range(8)], [f"ub{i}"], [f"bank{b}"])
            CP("act", uT[:, :, 1 + c * 128:1 + (c + 1) * 128], bkbf(b), [f"bank{b}"], [f"uT_{c}"])

        p1_a(0)
        for c in range(NCH):
            if c + 1 < NCH:
                p1_a(c + 1)
            p1_b(c)
        uT_all = [f"uT_{c}" for c in range(NCH)] + ["uT_h0", "uT_h1"]
        P.barrier()
        s1.close()

        s2 = ExitStack()
        sb2 = mk(s2)
        cosT = sb2("cosT", [128, LP]); sinT = sb2("sinT", [128, LP])
        DMA(cosT[:], cin["c_cos"], [], ["cosT"]); DMA(sinT[:], cin["c_sin"], [], ["sinT"])
        scw = sb2("scw", [128, 72]); scb = sb2("scb", [128, 24])
        DMA(scw[:], p_scw, [], ["scw"]); DMA(scb[:], p_scb, [], ["scb"])
        dtb_bc = sb2("dtb_bc", [128, 64])
        DMA(dtb_bc[:], r_dtb.partition_broadcast(128), [], ["dtb_bc"])
        wst = [sb2(f"wst{i}", [128, 8, 512]) for i in range(2)]
        wbf = [sb2(f"wbf{i}", [128, 8, 512], BF16) for i in range(2)]
        wctr = [0]

        wupf = [sb2("wupf0", [128, 2816])]
        wupst = [sb2("wupst0", [128, 2816], BF16)]
        WUPv = WUPb.rearrange("p k (j t c) -> p k j t c", j=22, t=2)

        def wup_iter(idx):
            k, t = idx // 2, idx % 2
            wi = 0
            DMA(wupf[wi][:], w_up[k * 128:(k + 1) * 128, t * 2816:(t + 1) * 2816], [], [f"wupf{wi}"])
            CP("pool", wupst[wi][:], wupf[wi][:], [f"wupf{wi}"], [f"wupst{wi}"])
            for (j0, j1) in [(0, 6), (6, 12), (12, 17), (17, 22)]:
                DMA(WUPv[:, k, j0:j1, t, :], wupst[wi][:].rearrange("p (a b) -> p a b", a=22)[:, j0:j1, :],
                    [f"wupst{wi}"], [f"WUPb{j0}"])

        def load_w(pieces, ncols):
            i = wctr[0] % 2
            if wctr[0] < 16:
                wup_iter(wctr[0])
            wctr[0] += 1
            for (d0, s0, n) in pieces:
                DMA(wst[i][:, :, d0:d0 + n], w_in[:, s0:s0 + n].rearrange("(k p) n -> p k n", p=128), [], [f"wst{i}"])
            CP("pool", wbf[i][:, :, 0:ncols], wst[i][:, :, 0:ncols], [f"wst{i}"], [f"wbf{i}"])
            return wbf[i], f"wbf{i}"

        t1 = [sb2(f"t1_{i}", [128, 512]) for i in range(2)]
        t2 = [sb2(f"t2_{i}", [128, 512]) for i in range(2)]
        NOB, NTRB, NOT, NDT = 5, 6, 6, 3
        ob = [sb2(f"ob{i}", [128, 512], BF16) for i in range(NOB)]
        trb = [sb2(f"trb{i}", [128, 4, 128], BF16) for i in range(NTRB)]
        ot = [sb2(f"ot{i}", [128, 512], BF16) for i in range(NOT)]
        dtt = [sb2(f"dtt{i}", [128, 64]) for i in range(NDT)]
        ctr = {"t": 0, "o": 0, "tr": 0, "ot": 0, "dt": 0}

        def transposes_to_tm(O, bo, n, dst_fn):
            nj = n // 128
            b = getbank()
            i = ctr["tr"] % NTRB
            ctr["tr"] += 1
            TR([(bkbf(b)[:, j, :], O[:, j * 128:(j + 1) * 128]) for j in range(nj)], [bo], [f"bank{b}"])
            CP("act", trb[i][:, 0:nj, :], bkbf(b)[:, 0:nj, :], [f"bank{b}"], [f"trb{i}"])
            ap, nm = dst_fn(nj)
            DMA(ap, trb[i][:, 0:nj, :], [f"trb{i}"], [nm])

        pend = []

        def flush():
            while pend:
                pend.pop(0)()

        jobs = []

        def qk_job(fam, DST, h):
            def body(W, bW):
                if stage2 or pend:
                    drain_job()(W, bW)
                for tb in range(9):
                    t0 = tb * 512
                    n = min(512, LP - t0)
                    ba, bb_ = getbank(), getbank()
                    MM([(bk(ba)[:, 0:n], W[:, k, 0:128], uT[:, k, 1 + t0:1 + t0 + n], k == 0, k == 7) for k in range(8)],
                       [bW] + uT_all, [f"bank{ba}"])
                    MM([(bk(bb_)[:, 0:n], W[:, k, 128:256], uT[:, k, 1 + t0:1 + t0 + n], k == 0, k == 7) for k in range(8)],
                       [bW] + uT_all, [f"bank{bb_}"])
                    flush()
                    i = ctr["t"] % 2
                    ctr["t"] += 1
                    io = ctr["o"] % NOB
                    ctr["o"] += 1
                    TT("dve", t1[i][:, 0:n], bk(ba)[:, 0:n], cosT[:, t0:t0 + n], ALU.mult, [f"bank{ba}", "cosT"], [f"t1_{i}"])
                    TT("dve", t2[i][:, 0:n], bk(bb_)[:, 0:n], sinT[:, t0:t0 + n], ALU.mult, [f"bank{bb_}", "sinT"], [f"t2_{i}"])
                    TT("pool" if tb % 3 == 2 else "dve", ob[io][:, 0:n], t1[i][:, 0:n], t2[i][:, 0:n], ALU.add,
                       [f"t1_{i}", f"t2_{i}"], [f"ob{io}"])
                    DMA(DST[h, :, t0:t0 + n], ob[io][:, 0:n], [f"ob{io}"], [f"{'QK'[fam]}T{h}_{tb}"])
                    if fam == 1:
                        pend.append(lambda io=io, n=n, t0=t0, h=h: transposes_to_tm(
                            ob[io], f"ob{io}", n,
                            lambda nj: (K_tm[t0:t0 + nj * 128, h * 128:(h + 1) * 128].rearrange("(j p) d -> p j d", p=128), "K_tm")))
            return body

        for fam, (col0, DST) in enumerate([(OQ, QT), (OK_, KT)]):
            for h in range(RH):
                c0 = col0 + h * 128
                jobs.append(([(0, c0, 128), (128, c0 + 64, 64), (192, c0, 64)], 256, qk_job(fam, DST, h)))

        stage2 = []

        def xbc_tail(m, tb, t0, i, io):
            T = t1[i]
            ACTV(ob[io][:, 0:384], T[:, 0:384], AF.Silu, [f"t1_{i}"], [f"ob{io}"])
            if tb == 0:
                MSET("pool", ob[io][:, 0:PAD], 0.0, [f"ob{io}"])
            flush()
            if m < 16:
                pend.append(lambda: transposes_to_tm(
                    ob[io], f"ob{io}", 384,
                    lambda nj: (XS_tm[t0:t0 + nj * 128, m * 128:(m + 1) * 128].rearrange("(j p) d -> p j d", p=128), "XS_tm")))
            elif m < 20:
                g = m - 16
                DMA(BT[g, :, t0:t0 + 384], ob[io][:, 0:384], [f"ob{io}"], [f"BT{g}_{tb}"])
                pend.append(lambda: transposes_to_tm(
                    ob[io], f"ob{io}", 384,
                    lambda nj: (B_tm[t0:t0 + nj * 128, g * 128:(g + 1) * 128].rearrange("(j p) d -> p j d", p=128), "B_tm")))
            else:
                g = m - 20
                DMA(CT[g, :, t0:t0 + 384], ob[io][:, 0:384], [f"ob{io}"], [f"CT{g}_{tb}"])

        def xbc_job(f4):
          def body(W, bW):
            for mm in range(4):
                m = f4 * 4 + mm
                for tb in range(11):
                    t0 = tb * 384
                    b = getbank()
                    MM([(bk(b)[:, 0:386], W[:, k, mm * 128:(mm + 1) * 128], uT[:, k, t0:t0 + 386], k == 0, k == 7) for k in range(8)],
                       [bW] + uT_all, [f"bank{b}"])
                    i = ctr["t"] % 2
                    ctr["t"] += 1
                    io = ctr["o"] % NOB
                    ctr["o"] += 1
                    T = t1[i]
                    ACTV(T[:, 0:384], bk(b)[:, 1:385], AF.Identity, [f"bank{b}", "scw", "scb"], [f"t1_{i}"],
                         scale=scw[:, m * 3 + 1:m * 3 + 2], bias=scb[:, m:m + 1])
                    STT(T[:, 0:384], bk(b)[:, 0:384], scw[:, m * 3:m * 3 + 1], T[:, 0:384], ALU.mult, ALU.add,
                        [f"bank{b}", f"t1_{i}"], [f"t1_{i}"])
                    STT(T[:, 0:384], bk(b)[:, 2:386], scw[:, m * 3 + 2:m * 3 + 3], T[:, 0:384], ALU.mult, ALU.add,
                        [f"bank{b}", f"t1_{i}"], [f"t1_{i}"])
                    while stage2:
                        stage2.pop(0)()
                    stage2.append(lambda m=m, tb=tb, t0=t0, i=i, io=io: xbc_tail(m, tb, t0, i, io))
          return body

        for f4 in range(6):
            jobs.append(([(0, OX + f4 * 512, 512)], 512, xbc_job(f4)))

        def drain_job():
            def body(W, bW):
                while stage2:
                    stage2.pop(0)()
                flush()
            return body
        def tm_family(col0, ncols, func, DST, dcol0, name):
          def body(W, bW):
            if stage2 or pend:
                drain_job()(W, bW)
            for c in range(NCH):
                b = getbank()
                MM([(bk(b)[:, 0:ncols], uT[:, k, 1 + c * 128:1 + (c + 1) * 128], W[:, k, 0:ncols], k == 0, k == 7) for k in range(8)],
                   [bW] + uT_all, [f"bank{b}"])
                i = ctr["ot"] % NOT
                ctr["ot"] += 1
                if func is None:
                    CP("act", ot[i][:, 0:ncols], bk(b)[:, 0:ncols], [f"bank{b}"], [f"ot{i}"])
                else:
                    ACTV(ot[i][:, 0:ncols], bk(b)[:, 0:ncols], func, [f"bank{b}"], [f"ot{i}"])
                DMA(DST[c * 128:(c + 1) * 128, dcol0:dcol0 + ncols], ot[i][:, 0:ncols], [f"ot{i}"], [name])
          jobs.append(([(0, col0, ncols)], ncols, body))

        for j in range(2):
            tm_family(OV + j * 512, 512, None, V_tm, j * 512, "V_tm")
        for j in range(2):
            tm_family(OG + j * 512, 512, AF.Silu, G_tm, j * 512, "G_tm")
        for j in range(4):
            tm_family(OZ + j * 512, 512, AF.Silu, Z_tm, j * 512, "Z_tm")
        for j in range(4):
            tm_family(OGR + j * 512, 512, AF.Sigmoid, GG_tm, j * 512, "GG_tm")
        def dt_body(W, bW):
          if stage2 or pend:
              drain_job()(W, bW)
          for c in range(NCH):
            b = getbank()
            MM([(bk(b)[:, 0:64], uT[:, k, 1 + c * 128:1 + (c + 1) * 128], W[:, k, 0:64], k == 0, k == 7) for k in range(8)],
               [bW] + uT_all, [f"bank{b}"])
            i = ctr["dt"] % NDT
            ctr["dt"] += 1
            TT("dve", dtt[i][:], bk(b)[:, 0:64], dtb_bc[:], ALU.add, [f"bank{b}", "dtb_bc"], [f"dtt{i}"])
            ACTV(dtt[i][:], dtt[i][:], AF.Exp, [f"dtt{i}"], [f"dtt{i}"])
            ACTV(dtt[i][:], dtt[i][:], AF.Ln, [f"dtt{i}"], [f"dtt{i}"], bias=1.0)
            if c == 0:
                MSET("pool", dtt[i][0:PAD, :], 0.0, [f"dtt{i}"])
            DMA(DT_tm[c * 128:(c + 1) * 128, :], dtt[i][:], [f"dtt{i}"], ["DT_tm"])
        jobs.append(([(0, ODF, 64)], 64, dt_body))
        assert len(jobs) == 27
        order = [0, 14, 1, 15, 2, 16, 3, 17, 4, 18, 5, 19, 6, 20, 7, 21, 8, 9, 10, 11, 12, 13, 22, 23, 24, 25, 26]
        jobs = [jobs[j] for j in order]
        loaded = {0: load_w(jobs[0][0], jobs[0][1])}
        for f in range(len(jobs)):
            if f + 1 < len(jobs):
                loaded[f + 1] = load_w(jobs[f + 1][0], jobs[f + 1][1])
            jobs[f][2](*loaded[f])
        drain_job()(None, None)
        P.barrier()
        s2.close()
        s12.close()
        if dbg == "p2":
            DMA(out[0:128, :], x_in[0:128, :], [], ["out"])
            P.emit(nc)
            return nc

        sFB = ExitStack()
        sbFB = mk(sFB)
        kdec = sbFB("kdec", [128, 8]); DMA(kdec[:], cin["c_kdec"], [], ["kdec"])
        tri = sbFB("tri", [128, 5, 128]); DMA(tri[:], cin["c_tri"], [], ["tri"])
        a_bc = sbFB("a_bc", [128, 64])
        DMA(a_bc[:], r_alog.partition_broadcast(128), [], ["a_bc"])
        ACTV(a_bc[:], a_bc[:], AF.Exp, ["a_bc"], ["a_bc"])
        TS("dve", a_bc[:], a_bc[:], -1.0, None, ALU.mult, None, ["a_bc"], ["a_bc"])
        CD = [g ** 128.0 for g in GAM]

        def v3(ap, a):
            return ap.rearrange("p (a b) -> p a b", a=a)

        def bc(ap2, n):
            return ap2.unsqueeze(2).to_broadcast([128, ap2.shape[1], n])

        sF = ExitStack()
        sbF = mk(sF)
        Rf = sbF("Rf", [128, 1024]); Sf = sbF("Sf", [128, 2048])
        Rf_bf = [sbF(f"Rf_bf{i}", [128, 1024], BF16) for i in range(2)]
        Sf_bf = [sbF(f"Sf_bf{i}", [128, 2048], BF16) for i in range(2)]
        MSET("pool", Rf[:], 0.0, ["Rf"]); MSET("pool", Sf[:], 0.0, ["Sf"])
        MSET("pool", Rf_bf[0][:], 0.0, ["Rf_bf0"]); MSET("pool", Sf_bf[0][:], 0.0, ["Sf_bf0"])
        fin = [dict(k=sbF(f"fk{i}", [128, 512], BF16), v=sbF(f"fv{i}", [128, 1024], BF16),
                    xs=sbF(f"fxs{i}", [128, 2048], BF16), b=sbF(f"fb{i}", [128, 512], BF16),
                    dt=sbF(f"fdt{i}", [128, 64])) for i in range(3)]
        kd = [sbF(f"pkd{i}", [128, 512], BF16) for i in range(3)]
        dta = [sbF(f"pdta{i}", [128, 32]) for i in range(3)]
        Ef = [sbF(f"pEf{i}", [128, 64]) for i in range(3)]
        w2 = [sbF(f"pw2{i}", [128, 32]) for i in range(3)]
        xdd = [sbF(f"pxdd{i}", [128, 2048], BF16) for i in range(3)]

        def pf_load(c):
            i = c % 3
            I = fin[i]
            rows = slice(c * 128, (c + 1) * 128)
            DMA(I["dt"][:], DT_tm[rows, :], ["DT_tm"], [f"fdt{i}"])
            DMA(I["k"][:], K_tm[rows, :], ["K_tm"], [f"fk{i}"])
            DMA(I["xs"][:], XS_tm[rows, :], ["XS_tm"], [f"fxs{i}"])
            DMA(I["v"][:], V_tm[rows, :], ["V_tm"], [f"fv{i}"])
            DMA(I["b"][:], B_tm[rows, :], ["B_tm"], [f"fb{i}"])

        def pf_pro(c):
            i = c % 3
            I = fin[i]
            TT("dve", v3(kd[i][:], 4), v3(I["k"][:], 4), bc(kdec[:, 0:4], 128), ALU.mult, [f"fk{i}", "kdec"], [f"pkd{i}"])
            TT("dve", dta[i][:], I["dt"][:, 0:32], a_bc[:, 0:32], ALU.mult, [f"fdt{i}", "a_bc"], [f"pdta{i}"])
            b = getbank()
            MM([(bk(b)[:, 0:32], tri[:, 4, :], dta[i][:], True, True), (bk(b)[:, 32:64], tri[:, 2, :], dta[i][:], True, True)],
               ["tri", f"pdta{i}"], [f"bank{b}"])
            ACTV(Ef[i][:], bk(b)[:, 0:64], AF.Exp, [f"bank{b}"], [f"pEf{i}"])
            TT("dve", w2[i][:], I["dt"][:, 0:32], Ef[i][:, 32:64], ALU.mult, [f"fdt{i}", f"pEf{i}"], [f"pw2{i}"])
            TT("pool", v3(xdd[i][:], 32), v3(I["xs"][:], 32), bc(w2[i][:], 64), ALU.mult, [f"fxs{i}", f"pw2{i}"], [f"pxdd{i}"])

        def pf_upd(c):
            i = c % 3
            j = c % 2
            I = fin[i]
            DMA(RFs[c], Rf_bf[j][:], [f"Rf_bf{j}"], [f"RFs{c}"])
            DMA(PFs[c], Sf_bf[j][:], [f"Sf_bf{j}"], [f"PFs{c}"])
            if c == NCH - 1:
                return
            TT("dve", v3(Sf[:], 32), v3(Sf[:], 32), bc(Ef[i][:, 0:32], 64), ALU.mult, ["Sf", f"pEf{i}"], ["Sf"])
            for hp in range(2):
                b = getbank()
                MM([(bk(b)[:, hh * 256:(hh + 1) * 256], kd[i][:, (2 * hp + hh) * 128:(2 * hp + hh + 1) * 128],
                     I["v"][:, (2 * hp + hh) * 256:(2 * hp + hh + 1) * 256], True, True) for hh in range(2)],
                   [f"pkd{i}", f"fv{i}"], [f"bank{b}"])
                for hh in range(2):
                    h = 2 * hp + hh
                    STT(Rf[:, h * 256:(h + 1) * 256], Rf[:, h * 256:(h + 1) * 256], CD[h], bk(b)[:, hh * 256:(hh + 1) * 256],
                        ALU.mult, ALU.add, ["Rf", f"bank{b}"], ["Rf"])
            CP("act", Rf_bf[1 - j][:], Rf[:], ["Rf"], [f"Rf_bf{1 - j}"])
            for g in range(4):
                b = getbank()
                MM([(bk(b), I["b"][:, g * 128:(g + 1) * 128], xdd[i][:, g * 512:(g + 1) * 512], True, True)],
                   [f"fb{i}", f"pxdd{i}"], [f"bank{b}"])
                sg_ = Sf[:, g * 512:(g + 1) * 512]
                TT("dve", sg_, sg_, bk(b), ALU.add, ["Sf", f"bank{b}"], ["Sf"])
            CP("act", Sf_bf[1 - j][:], Sf[:], ["Sf"], [f"Sf_bf{1 - j}"])

        pf_load(0)
        pf_load(1)
        pf_pro(0)
        pf_pro(1)
        for c in range(NCH):
            if c + 2 < NCH - 1:
                pf_load(c + 2)
                pf_pro(c + 2)
            pf_upd(c)
        P.barrier()
        sF.close()
        if dbg == "pf":
            DMA(out[0:128, :], x_in[0:128, :], [], ["out"])
            P.emit(nc)
            return nc

        sB = ExitStack()
        sbB = mk(sB)
        dmT = sbB("dmT", [128, 4, 128]); DMA(dmT[:], cin["c_dmT"], [], ["dmT"])
        qdf = sbB("qdf", [128, 4, 128]); DMA(qdf[:], cin["c_qdf"], [], ["qdf"])
        qdb = sbB("qdb", [128, 4, 128]); DMA(qdb[:], cin["c_qdb"], [], ["qdb"])
        mask = sbB("mask", [128, 2, 128]); DMA(mask[:], cin["c_mask"], [], ["mask"])
        negm = sbB("negm", [128, 2, 512], BF16)
        DMA(stg[:, 0:1024], cin["c_negm"].rearrange("p a b -> p (a b)"), ["cstage"], ["cstage"])
        CP("dve", negm[:].rearrange("p a b -> p (a b)"), stg[:, 0:1024], ["cstage"], ["negm"])
        sel = sbB("sel", [96, 4096], BF16)
        DMA(stg[0:96, 0:4096], cin["c_sel"], ["cstage"], ["cstage"])
        CP("dve", sel[:], stg[0:96, 0:4096], ["cstage"], ["sel"])
        dsk_bc = sbB("dsk_bc", [128, 32]); DMA(dsk_bc[:], r_dskip.partition_broadcast(128), [], ["dsk_bc"])
        Rb = sbB("Rb", [128, 1024]); Sb = sbB("Sb", [128, 2048])
        Rb_bf = sbB("Rb_bf", [128, 1024], BF16); Sb_bf = sbB("Sb_bf", [128, 2048], BF16)
        MSET("pool", Rb[:], 0.0, ["Rb"]); MSET("pool", Sb[:], 0.0, ["Sb"])
        MSET("pool", Rb_bf[:], 0.0, ["Rb_bf"]); MSET("pool", Sb_bf[:], 0.0, ["Sb_bf"])
        rinp = [dict(qT=sbB(f"bqT{i}", [128, 4, 128], BF16), kT=sbB(f"bkT{i}", [128, 4, 128], BF16),
                     k=sbB(f"bk{i}", [128, 512], BF16), v=sbB(f"bv{i}", [128, 1024], BF16),
                     g=sbB(f"bg{i}", [128, 1024], BF16), rf=sbB(f"brf{i}", [128, 1024], BF16)) for i in range(2)]
        sinp = [dict(xs=sbB(f"bxs{i}", [128, 2048], BF16), bT=sbB(f"bbT{i}", [128, 4, 128], BF16),
                     cT=sbB(f"bcT{i}", [128, 4, 128], BF16), b=sbB(f"bb{i}", [128, 512], BF16),
                     dt=sbB(f"bdt{i}", [128, 64]),
                     pf=sbB(f"bpf{i}", [128, 2048], BF16)) for i in range(2)]
        zin = [sbB(f"bz{i}", [128, 2048], BF16) for i in range(3)]
        SD = sbB("SD", [128, 4, 128], BF16); qf = sbB("qf", [128, 4, 128], BF16); qb = sbB("qb", [128, 4, 128], BF16)
        yr = sbB("yr", [128, 1024]); kdb = sbB("kdb", [128, 512], BF16)
        st6 = sbB("st6", [128, 4, 6]); mv = sbB("mv", [128, 4, 2]); rstd = sbB("rstd", [128, 4]); lnt = sbB("lnt", [128, 4])
        lnt2 = sbB("lnt2", [128, 4])
        yg = sbB("yg", [128, 1024], BF16); ygT = sbB("ygT", [128, 8, 128], BF16)
        dta2 = sbB("dta2", [128, 64])
        acs = [sbB(f"acs{i}", [128, 64]) for i in range(2)]
        E = [sbB(f"E{i}", [128, 160]) for i in range(2)]
        A3 = sbB("A3", [128, 2, 96], BF16); r1 = sbB("r1", [128, 2, 32]); r2 = sbB("r2", [128, 2, 32])
        aT3 = [sbB(f"aT3_{i}", [96, 2, 128], BF16) for i in range(2)]
        naT3 = [sbB(f"naT3_{i}", [96, 2, 128], BF16) for i in range(2)]
        cbm = [sbB(f"cbm{i}", [128, 2, 512], BF16) for i in range(2)]
        xd = [[sbB(f"xd{i}_{d}", [128, 2048], BF16) for d in range(2)] for i in range(2)]
        xsd = [sbB(f"xsd{i}", [128, 2048], BF16) for i in range(2)]
        w2b = sbB("w2b", [128, 32]); xddb = sbB("xddb", [128, 2048], BF16)
        NL = 4
        Lq = [sbB(f"Lq{i}", [128, 512], BF16) for i in range(NL)]
        Mq = [sbB(f"Mq{i}", [128, 4, 128], BF16) for i in range(NL)]
        toff = [sbB(f"toff{i}", [128, 512]) for i in range(2)]
        Ysb2 = [sbB(f"Ysb{i}", [128, 2048]) for i in range(2)]; ssg = sbB("ssg", [128, 4]); rs4 = sbB("rs4", [128, 4]); junkb = sbB("junkb", [128, 512])
        ynb = sbB("ynb", [128, 2048], BF16); yT = sbB("yT", [128, 16, 128], BF16)
        ARG = [2, 3, 4]
        rr = {"s": 0, "r": 0, "y": 0, "l": 0}

        def sbank():
            return 5

        def rbank():
            rr["r"] += 1
            return 6 + rr["r"] % 2

        def load_r(c):
            i = c % 2
            I = rinp[i]
            rows = slice(c * 128, (c + 1) * 128)
            DMA(I["qT"][:], QT[:, :, rows].rearrange("h d t -> d h t"), [f"QT{h}_{c // 4}" for h in range(4)], [f"bqT{i}"])
            DMA(I["kT"][:], KT[:, :, rows].rearrange("h d t -> d h t"), [f"KT{h}_{c // 4}" for h in range(4)], [f"bkT{i}"])
            DMA(I["k"][:], K_tm[rows, :], ["K_tm"], [f"bk{i}"])
            DMA(I["v"][:], V_tm[rows, :], ["V_tm"], [f"bv{i}"])
            DMA(I["g"][:], G_tm[rows, :], ["G_tm"], [f"bg{i}"])
            DMA(I["rf"][:], RFs[c], [f"RFs{c}"], [f"brf{i}"])

        def load_s(c):
            i = c % 2
            I = sinp[i]
            rows = slice(c * 128, (c + 1) * 128)
            DMA(I["dt"][:], DT_tm[rows, :], ["DT_tm"], [f"bdt{i}"])
            DMA(I["xs"][:], XS_tm[rows, :], ["XS_tm"], [f"bxs{i}"])
            DMA(I["bT"][:], BT[:, :, rows].rearrange("h d t -> d h t"), [f"BT{h}_{c // 3}" for h in range(4)], [f"bbT{i}"])
            DMA(I["cT"][:], CT[:, :, rows].rearrange("h d t -> d h t"), [f"CT{h}_{c // 3}" for h in range(4)], [f"bcT{i}"])
            DMA(I["b"][:], B_tm[rows, :], ["B_tm"], [f"bb{i}"])
            DMA(zin[c % 3][:], Z_tm[rows, :], ["Z_tm"], [f"bz{c % 3}"])
            DMA(I["pf"][:], PFs[c], [f"PFs{c}"], [f"bpf{i}"])

        def emit_R(c):
            i = c % 2
            I = rinp[i]
            n_ = lambda s: f"b{s}{i}"
            bS = rbank()
            MM([(bk(bS)[:, h * 128:(h + 1) * 128], I["kT"][:, h, :], I["qT"][:, h, :], True, True) for h in range(4)],
               [n_("kT"), n_("qT")], [f"bank{bS}"])
            TT("dve", SD[:], v3(bk(bS), 4), dmT[:], ALU.mult, [f"bank{bS}", "dmT"], ["SD"])
            TT("pool", qf[:], I["qT"][:], qdf[:], ALU.mult, [n_("qT"), "qdf"], ["qf"])
            TT("pool", qb[:], I["qT"][:], qdb[:], ALU.mult, [n_("qT"), "qdb"], ["qb"])
            for hp in range(2):
                b = rbank()
                lst = []
                for hh in range(2):
                    h = 2 * hp + hh
                    o = bk(b)[:, hh * 256:(hh + 1) * 256]
                    lst += [(o, SD[:, h, :], I["v"][:, h * 256:(h + 1) * 256], True, False),
                            (o, qf[:, h, :], I["rf"][:, h * 256:(h + 1) * 256], False, False),
                            (o, qb[:, h, :], Rb_bf[:, h * 256:(h + 1) * 256], False, True)]
                MM(lst, ["SD", "qf", "qb", n_("v"), n_("rf"), "Rb_bf"], [f"bank{b}"])
                CP("act", yr[:, hp * 512:(hp + 1) * 512], bk(b), [f"bank{b}"], ["yr"])
            TT("dve", v3(kdb[:], 4), v3(I["k"][:], 4), bc(kdec[:, 4:8], 128), ALU.mult, [n_("k"), "kdec"], ["kdb"])
            for hp in range(2):
                b = rbank()
                MM([(bk(b)[:, hh * 256:(hh + 1) * 256], kdb[:, (2 * hp + hh) * 128:(2 * hp + hh + 1) * 128],
                     I["v"][:, (2 * hp + hh) * 256:(2 * hp + hh + 1) * 256], True, True) for hh in range(2)],
                   ["kdb", n_("v")], [f"bank{b}"])
                for hh in range(2):
                    h = 2 * hp + hh
                    STT(Rb[:, h * 256:(h + 1) * 256], Rb[:, h * 256:(h + 1) * 256], CD[h], bk(b)[:, hh * 256:(hh + 1) * 256],
                        ALU.mult, ALU.add, ["Rb", f"bank{b}"], ["Rb"])
            CP("act", Rb_bf[:], Rb[:], ["Rb"], ["Rb_bf"])
            for h in range(4):
                P.op("dve", lambda e, h=h: e.bn_stats(out=st6[:, h, :], in_=yr[:, h * 256:(h + 1) * 256]), ["yr"], ["st6"])
            for h in range(4):
                P.op("dve", lambda e, h=h: e.bn_aggr(out=mv[:, h, :], in_=st6[:, h, :]), ["st6"], ["mv"])
            ACTV(lnt[:], mv[:, :, 1], AF.Ln, ["mv"], ["lnt"], bias=EPS)
            ACTV(rstd[:], lnt[:], AF.Exp, ["lnt"], ["rstd"], scale=-0.5)
            for h in range(4):
                TS("dve", yr[:, h * 256:(h + 1) * 256], yr[:, h * 256:(h + 1) * 256], mv[:, h, 0:1], rstd[:, h:h + 1],
                   ALU.subtract, ALU.mult, ["yr", "mv", "rstd"], ["yr"])
            TT("dve", yg[:], yr[:], I["g"][:], ALU.mult, ["yr", n_("g")], ["yg"])
            b = rbank()
            TR([(bkbf(b)[:, k, :], yg[:, k * 128:(k + 1) * 128]) for k in range(8)], ["yg"], [f"bank{b}"])
            CP("act", ygT[:], bkbf(b), [f"bank{b}"], ["ygT"])
            DMA(YRT[c], ygT[:], ["ygT"], [f"YRT{c}"])

        def emit_pro(c):
            i = c % 2
            I = sinp[i]
            n_ = lambda s: f"b{s}{i}"
            TT("dve", dta2[:], I["dt"][:], a_bc[:], ALU.mult, [n_("dt"), "a_bc"], ["dta2"])
            bA = rbank()
            MM([(bk(bA)[:, 0:32], tri[:, 0, :], dta2[:, 0:32], True, True),
                (bk(bA)[:, 32:64], tri[:, 1, :], dta2[:, 32:64], True, True),
                (bk(bA)[:, 64:96], tri[:, 4, :], dta2[:, 0:32], True, True),
                (bk(bA)[:, 96:128], tri[:, 4, :], dta2[:, 32:64], True, True),
                (bk(bA)[:, 128:160], tri[:, 3, :], dta2[:, 32:64], True, True)], ["tri", "dta2"], [f"bank{bA}"])
            CP("act", acs[i][:], bk(bA)[:, 0:64], [f"bank{bA}"], [f"acs{i}"])
            ACTV(E[i][:], bk(bA)[:, 0:160], AF.Exp, [f"bank{bA}"], [f"E{i}"])
            acv = v3(acs[i][:], 2)
            CP("dve", A3[:, :, 0:32], acv, [f"acs{i}"], ["A3"])
            TT("dve", r1[:], acv, A3[:, :, 0:32], ALU.subtract, [f"acs{i}", "A3"], ["r1"])
            CP("dve", A3[:, :, 32:64], r1[:], ["r1"], ["A3"])
            TT("dve", r2[:], r1[:], A3[:, :, 32:64], ALU.subtract, ["r1", "A3"], ["r2"])
            CP("dve", A3[:, :, 64:96], r2[:], ["r2"], ["A3"])
            b = rbank()
            TR([(bkbf(b)[0:96, d, :], A3[:, d, :]) for d in range(2)], ["A3"], [f"bank{b}"])
            CP("act", aT3[i][:], bkbf(b)[0:96, 0:2, :], [f"bank{b}"], [f"aT3_{i}"])
            ACTV(naT3[i][:], bkbf(b)[0:96, 0:2, :], AF.Identity, [f"bank{b}"], [f"naT3_{i}"], scale=-1.0)
            bC = rbank()
            MM([(bk(bC)[:, g * 128:(g + 1) * 128], I["bT"][:, g, :], I["cT"][:, g, :], True, True) for g in range(4)],
               [n_("bT"), n_("cT")], [f"bank{bC}"])
            for d in range(2):
                TT("dve", v3(cbm[i][:, d, :], 4), v3(bk(bC), 4), mask[:, d:d + 1, :].to_broadcast([128, 4, 128]), ALU.mult,
                   [f"bank{bC}", "mask"], [f"cbm{i}"])
            TT("pool", v3(xsd[i][:], 32), v3(I["xs"][:], 32), bc(dsk_bc[:], 64), ALU.mult, [n_("xs"), "dsk_bc"], [f"xsd{i}"])
            for d in range(2):
                TT("pool", v3(xd[i][d][:], 32), v3(I["xs"][:], 32), bc(I["dt"][:, d * 32:(d + 1) * 32], 64), ALU.mult,
                   [n_("xs"), n_("dt")], [f"xd{i}_{d}"])

        def emit_main(c):
            i = c % 2
            I = sinp[i]
            n_ = lambda s: f"b{s}{i}"
            Ysb = Ysb2[i]
            nYsb = f"Ysb{i}"
            iters = [(g, d, bq) for g in range(4) for d in range(2) for bq in (2 * g, 2 * g + 1)]
            ybank = {}

            def A(k):
                g, d, bq = iters[k]
                bL = ARG[k % 3]
                lst = [(bk(bL), ident[:], negm[:, d, :], True, False),
                       (bk(bL), naT3[i][:, d, :], sel[:, bq * 512:(bq + 1) * 512], False, False)]
                for hh in range(4):
                    h = 4 * bq + hh
                    lst.append((bk(bL)[:, hh * 128:(hh + 1) * 128], sel[:, h * 128:(h + 1) * 128], aT3[i][:, d, :], False, hh == 3))
                MM(lst, ["ident", "negm", f"naT3_{i}", f"aT3_{i}", "sel"], [f"bank{bL}"])

            slot = {}

            def EM(k):
                g, d, bq = iters[k]
                bL = ARG[k % 3]
                li = rr["l"] % NL
                rr["l"] += 1
                slot[k] = li
                ACTV(Lq[li][:], bk(bL), AF.Exp, [f"bank{bL}"], [f"Lq{li}"])
                TT("dve", Mq[li][:], v3(Lq[li][:], 4),
                   cbm[i][:, d, g * 128:(g + 1) * 128].unsqueeze(1).to_broadcast([128, 4, 128]), ALU.mult,
                   [f"Lq{li}", f"cbm{i}"], [f"Mq{li}"])

            def YM(k):
                g, d, bq = iters[k]
                li = slot[k]
                lst = []
                yb = ybank[g]
                for hh in range(4):
                    h = 4 * bq + hh
                    last = (k % 4 == 3 and hh == 3)
                    lst.append((bk(yb)[:, (h % 8) * 64:(h % 8 + 1) * 64], Mq[li][:, hh, :], xd[i][d][:, h * 64:(h + 1) * 64], False, last))
                MM(lst, [f"Mq{li}", f"xd{i}_{d}"], [f"bank{yb}"])

            def group_tail(g):
                yb = ybank[g]
                for d2 in range(2):
                    bO = sbank()
                    prev = I["pf"] if d2 == 0 else Sb_bf
                    MM([(bk(bO), I["cT"][:, g, :], prev[:, g * 512:(g + 1) * 512], True, True)],
                       [n_("cT"), n_("pf") if d2 == 0 else "Sb_bf"], [f"bank{bO}"])
                    TT("dve", v3(toff[d2][:], 8), v3(bk(bO), 8), bc(E[i][:, d2 * 32 + g * 8:d2 * 32 + (g + 1) * 8], 64), ALU.mult,
                       [f"bank{bO}", f"E{i}"], [f"toff{d2}"])
                yg_ = Ysb[:, g * 512:(g + 1) * 512]
                TT("dve", yg_, bk(yb), toff[0][:], ALU.add, [f"bank{yb}", "toff0"], [nYsb])
                TT("pool", yg_, yg_, toff[1][:], ALU.add, [nYsb, "toff1"], [nYsb])

            A(0)
            A(1)
            for k in range(17):
                if k < 16:
                    g, d, bq = iters[k]
                    if k % 4 == 0:
                        yb = rr["y"] % 2
                        rr["y"] += 1
                        ybank[g] = yb
                        MM([(bk(yb), ident[:], xsd[i][:, g * 512:(g + 1) * 512], True, False)], ["ident", f"xsd{i}"], [f"bank{yb}"])
                    if k + 2 < 16:
                        A(k + 2)
                    EM(k)
                if k >= 1:
                    YM(k - 1)
                    if (k - 1) % 4 == 3:
                        group_tail(iters[k - 1][0])
                if False:
                    yb = ybank[g]
                    for d2 in range(2):
                        bO = sbank()
                        prev = I["pf"] if d2 == 0 else Sb_bf
                        MM([(bk(bO), I["cT"][:, g, :], prev[:, g * 512:(g + 1) * 512], True, True)],
                           [n_("cT"), n_("pf") if d2 == 0 else "Sb_bf"], [f"bank{bO}"])
                        TT("dve", v3(toff[d2][:], 8), v3(bk(bO), 8), bc(E[i][:, d2 * 32 + g * 8:d2 * 32 + (g + 1) * 8], 64), ALU.mult,
                           [f"bank{bO}", f"E{i}"], [f"toff{d2}"])
                    yg_ = Ysb[:, g * 512:(g + 1) * 512]
                    TT("dve", yg_, bk(yb), toff[0][:], ALU.add, [f"bank{yb}", "toff0"], ["Ysb"])
                    TT("pool", yg_, yg_, toff[1][:], ALU.add, ["Ysb", "toff1"], ["Ysb"])
                    if P.cap is not None:
                        P.cap.append(("mark",))
            TT("dve", w2b[:], I["dt"][:, 32:64], E[i][:, 128:160], ALU.mult, [n_("dt"), f"E{i}"], ["w2b"])
            TT("pool", v3(xddb[:], 32), v3(I["xs"][:], 32), bc(w2b[:], 64), ALU.mult, [n_("xs"), "w2b"], ["xddb"])
            TT("pool", v3(Sb[:], 32), v3(Sb[:], 32), bc(E[i][:, 96:128], 64), ALU.mult, ["Sb", f"E{i}"], ["Sb"])
            for g in range(4):
                b = sbank()
                MM([(bk(b), I["b"][:, g * 128:(g + 1) * 128], xddb[:, g * 512:(g + 1) * 512], True, True)],
                   [n_("b"), "xddb"], [f"bank{b}"])
                sg_ = Sb[:, g * 512:(g + 1) * 512]
                TT("dve", sg_, sg_, bk(b), ALU.add, ["Sb", f"bank{b}"], ["Sb"])
            CP("act", Sb_bf[:], Sb[:], ["Sb"], ["Sb_bf"])
        def emit_tailB(c, bankfn):
            i = c % 2
            Ysb = Ysb2[i]
            nYsb = f"Ysb{i}"
            zt, nz = zin[c % 3], f"bz{c % 3}"
            TT("pool", Ysb[:], Ysb[:], zt[:], ALU.mult, [nYsb, nz], [nYsb])
            for g in range(4):
                ACTV(junkb[:], Ysb[:, g * 512:(g + 1) * 512], AF.Square, [nYsb], ["junkb", "ssg"], accum=ssg[:, g:g + 1])
            ACTV(lnt2[:], ssg[:], AF.Ln, ["ssg"], ["lnt2"], scale=1.0 / 512, bias=EPS)
            ACTV(rs4[:], lnt2[:], AF.Exp, ["lnt2"], ["rs4"], scale=-0.5)
            for g in range(4):
                ACTV(ynb[:, g * 512:(g + 1) * 512], Ysb[:, g * 512:(g + 1) * 512], AF.Identity, [nYsb, "rs4"], ["ynb"],
                     scale=rs4[:, g:g + 1])
            for hf in range(2):
                b = bankfn()
                TR([(bkbf(b)[:, k, :], ynb[:, (hf * 8 + k) * 128:(hf * 8 + k + 1) * 128]) for k in range(8)], ["ynb"], [f"bank{b}"])
                CP("act", yT[:, hf * 8:(hf + 1) * 8, :], bkbf(b), [f"bank{b}"], ["yT"])
            DMA(YST[c], yT[:], ["yT"], [f"YST{c}"])

        load_r(NCH - 1)
        load_s(NCH - 1)
        emit_pro(NCH - 1)
        for c in range(NCH - 1, -1, -1):
            if c > 0:
                load_s(c - 1)
                load_r(c - 1)
            P.begin_capture(); emit_main(c); emit_tailB(c, sbank); Lm = P.end_capture()
            Lm2 = [it for it in Lm if it[0] != "mark"]
            P.begin_capture()
            emit_R(c)
            if c > 0:
                emit_pro(c - 1)
            L2 = P.end_capture()
            P.replay(merge_threads([Lm2, L2], [0.0, 0.02], [1.0, 0.95]))
        P.barrier()
        sB.close()
        sFB.close()
        if dbg == "pb":
            DMA(out[0:128, :], x_in[0:128, :], [], ["out"])
            P.emit(nc)
            return nc

        sC = ExitStack()
        sbC = mk(sC)
        u2T = sbC("u2T", [128, 8, LP + 2], BF16)
        MSET("pool", u2T[:, :, 0:1], 0.0, ["u2T_h0"])
        MSET("pool", u2T[:, :, LP + 1:LP + 2], 0.0, ["u2T_h1"])
        sC1 = ExitStack()
        sbC1 = mk(sC1)
        wro = sbC1("wro", [128, 8, 1024], BF16); wso = sbC1("wso", [128, 16, 1024], BF16); wo = sbC1("wo", [128, 8, 1024], BF16)
        ncol = sbC1("ncol", [128, 24])
        DMA(ncol[:], p_ncol, [], ["ncol"])
        ce = [0]
        for (wt, wn, src, nk, c0) in [(wro, "wro", w_ret_out, 8, 0), (wso, "wso", w_ssd_out, 16, 8), (wo, "wo", w_out, 8, None)]:
            for k2 in range(nk // 2):
                hf = ce[0] % 2
                sv = v3(stg[:, hf * 2048:(hf + 1) * 2048], 2)
                DMA(sv, src[k2 * 256:(k2 + 1) * 256, :].rearrange("(k p) n -> p k n", p=128), [], [f"cstage_h{hf}"])
                for kk in range(2):
                    k = k2 * 2 + kk
                    eng = ["act", "dve", "pool"][ce[0] % 3]
                    ce[0] += 1
                    if c0 is None:
                        CP(eng, wt[:, k, :], sv[:, kk, :], [f"cstage_h{hf}"], [wn])
                    elif eng == "act":
                        ACTV(wt[:, k, :], sv[:, kk, :], AF.Identity, [f"cstage_h{hf}", "ncol"], [wn], scale=ncol[:, c0 + k:c0 + k + 1])
                    else:
                        TS(eng, wt[:, k, :], sv[:, kk, :], ncol[:, c0 + k:c0 + k + 1], None, ALU.mult, None, [f"cstage_h{hf}", "ncol"], [wn])
                ce[0] += 1
        nfw_bc = sbC1("nfw_bc", [128, D]); DMA(nfw_bc[:], r_nfw.partition_broadcast(128), [], ["nfw_bc"])
        cinp = [dict(yr=sbC1(f"cyr{i}", [128, 8, 128], BF16), ys=sbC1(f"cys{i}", [128, 16, 128], BF16),
                     gg=sbC1(f"cgg{i}", [128, 2048], BF16)) for i in range(2)]
        m1 = sbC1("m1", [128, 512]); m2 = sbC1("m2", [128, 512])
        mg = [sbC1(f"mg{i}", [128, 1024], BF16) for i in range(2)]
        mT = sbC1("mT", [128, 8, 128], BF16)
        h2 = [sbC1(f"h2_{i}", [128, D]) for i in range(3)]
        junkc = stg[:, 0:D]
        u2 = [sbC1(f"u2_{i}", [128, D], BF16) for i in range(2)]
        ssc = [sbC1(f"ssc{i}", [128, 1]) for i in range(2)]
        rsc = [sbC1(f"rsc{i}", [128, 1]) for i in range(2)]

        def pc1_load(c):
            i = c % 2
            I = cinp[i]
            hi = c % 3
            rows = slice(c * 128, (c + 1) * 128)
            DMA(I["yr"][:], YRT[c], [f"YRT{c}"], [f"cyr{i}"])
            DMA(I["ys"][:], YST[c], [f"YST{c}"], [f"cys{i}"])
            DMA(I["gg"][:], GG_tm[rows, :], ["GG_tm"], [f"cgg{i}"])
            if c == 0:
                MSET("pool", h2[hi][:], 0.0, [f"h2_{hi}"])
                DMA(h2[hi][PAD:128, :], meta_in, [], [f"h2_{hi}"])
            else:
                DMA(h2[hi][:], x_in[(c - 1) * 128:c * 128, :], [], [f"h2_{hi}"])

        def pc1_A(c):
            i = c % 2
            I = cinp[i]
            for hf in range(2):
                cs = slice(hf * 512, (hf + 1) * 512)
                bR, bS_ = getbank(), getbank()
                MM([(bk(bR), I["yr"][:, k, :], wro[:, k, cs], k == 0, k == 7) for k in range(8)], [f"cyr{i}", "wro"], [f"bank{bR}"])
                MM([(bk(bS_), I["ys"][:, k, :], wso[:, k, cs], k == 0, k == 15) for k in range(16)], [f"cys{i}", "wso"], [f"bank{bS_}"])
                TT("dve", m1[:], bk(bR), I["gg"][:, hf * 512:(hf + 1) * 512], ALU.mult, [f"bank{bR}", f"cgg{i}"], ["m1"])
                TT("dve", m2[:], bk(bS_), I["gg"][:, 1024 + hf * 512:1024 + (hf + 1) * 512], ALU.mult, [f"bank{bS_}", f"cgg{i}"], ["m2"])
                TT("pool", mg[i][:, cs], m1[:], m2[:], ALU.add, ["m1", "m2"], [f"mg{i}"])

        def pc1_B(c):
            i = c % 2
            hi = c % 3
            rows = slice(c * 128, (c + 1) * 128)
            b = getbank()
            TR([(bkbf(b)[:, k, :], mg[i][:, k * 128:(k + 1) * 128]) for k in range(8)], [f"mg{i}"], [f"bank{b}"])
            CP("act", mT[:], bkbf(b), [f"bank{b}"], ["mT"])
            for hf in range(2):
                cs = slice(hf * 512, (hf + 1) * 512)
                b = getbank()
                MM([(bk(b), mT[:, k, :], wo[:, k, cs], k == 0, k == 7) for k in range(8)], ["mT", "wo"], [f"bank{b}"])
                TT("dve", h2[hi][:, cs], bk(b), h2[hi][:, cs], ALU.add, [f"bank{b}", f"h2_{hi}"], [f"h2_{hi}"])
            DMA(H2[rows, :], h2[hi][:], [f"h2_{hi}"], [f"H2_{c}"])
            ACTV(junkc, h2[hi][:], AF.Square, [f"h2_{hi}"], ["cstage", "cstage_h0", f"ssc{i}"], accum=ssc[i][:])
            ACTV(rsc[i][:], ssc[i][:], AF.Sqrt, [f"ssc{i}"], [f"rsc{i}"], scale=1.0 / D, bias=EPS)
            P.op("dve", lambda e: e.reciprocal(out=rsc[i][:], in_=rsc[i][:]), [f"rsc{i}"], [f"rsc{i}"])
            STT(u2[i][:], h2[hi][:], rsc[i][:], nfw_bc[:], ALU.mult, ALU.mult, [f"h2_{hi}", f"rsc{i}", "nfw_bc"], [f"u2_{i}"])

        def pc1_C(c):
            i = c % 2
            b = getbank()
            TR([(bkbf(b)[:, k, :], u2[i][:, k * 128:(k + 1) * 128]) for k in range(8)], [f"u2_{i}"], [f"bank{b}"])
            CP("act", u2T[:, :, 1 + c * 128:1 + (c + 1) * 128], bkbf(b), [f"bank{b}"], [f"u2T_{c}"])

        pc1_load(0)
        for t in range(NCH + 2):
            if t + 1 < NCH:
                pc1_load(t + 1)
            if t < NCH:
                pc1_A(t)
            if 0 <= t - 1 < NCH:
                pc1_B(t - 1)
            if 0 <= t - 2 < NCH:
                pc1_C(t - 2)
        u2T_all = [f"u2T_{c}" for c in range(NCH)] + ["u2T_h0", "u2T_h1"]
        P.barrier()
        sC1.close()
        if dbg == "pc1":
            DMA(out[0:128, :], x_in[0:128, :], [], ["out"])
            P.emit(nc)
            sC.close()
            return nc

        sD = ExitStack()
        sbD = mk(sD)
        wdn = sbD("wdn", [128, 22, 1024], BF16)
        stgs2 = [(stg, "cstage"), (stg, "cstage")]
        for j4 in range(6):
            nj = 4 if j4 < 5 else 2
            s, sn = stgs2[j4 % 2]
            DMA(v3(s[:, 0:nj * 1024], nj), w_down[j4 * 512:j4 * 512 + nj * 128, :].rearrange("(k p) n -> p k n", p=128), [], [sn])
            for kk in range(nj):
                CP(["act", "dve", "pool"][(j4 * 4 + kk) % 3], wdn[:, j4 * 4 + kk, :], v3(s[:, 0:nj * 1024], nj)[:, kk, :], [sn], ["wdn"])
        fcw = sbD("fcw", [128, 132]); fcb = sbD("fcb", [128, 44])
        DMA(fcw[:], p_fcw, [], ["fcw"]); DMA(fcb[:], p_fcb, [], ["fcb"])
        fnw_bc = sbD("fnw_bc", [128, D]); DMA(fnw_bc[:], r_fnw.partition_broadcast(128), [], ["fnw_bc"])
        actT = sbD("actT", [128, 22, 512], BF16)
        wp = [sbD(f"wp{i}", [128, 8, 256], BF16) for i in range(3)]
        cg = [sbD(f"cg{i}", [128, 256]) for i in range(2)]
        cu = [sbD(f"cu{i}", [128, 256]) for i in range(2)]
        sgt = [sbD(f"sgt{i}", [128, 256]) for i in range(2)]
        h3 = [sbD(f"h3_{i}", [128, D]) for i in range(2)]
        ot2 = [sbD(f"ot2_{i}", [128, D]) for i in range(2)]
        junkd = stg[:, 0:D]
        ssd_ = [sbD(f"ssd{i}", [128, 1]) for i in range(2)]
        rsd = [sbD(f"rsd{i}", [128, 1]) for i in range(2)]
        WUPp = WUPb.rearrange("p k (j c) -> p k j c", j=22)
        pc = [0]
        cc = [0]
        tcn = [0]
        for bI in range(8):
            T0 = 128 + bI * 512
            for j in range(22):
                wi = pc[0] % 3
                pc[0] += 1
                DMA(wp[wi][:], WUPp[:, :, j, :], ["WUPb0", "WUPb6", "WUPb12", "WUPb17"], [f"wp{wi}"])
                for sub in range(2):
                    ts = T0 + sub * 256
                    ci = cc[0] % 2
                    cc[0] += 1
                    bg, bu = getbank(), getbank()
                    MM([(bk(bg)[:, 0:258], wp[wi][:, k, 0:128], u2T[:, k, ts:ts + 258], k == 0, k == 7) for k in range(8)],
                       [f"wp{wi}"] + u2T_all, [f"bank{bg}"])
                    MM([(bk(bu)[:, 0:258], wp[wi][:, k, 128:256], u2T[:, k, ts:ts + 258], k == 0, k == 7) for k in range(8)],
                       [f"wp{wi}"] + u2T_all, [f"bank{bu}"])
                    for (bb2, T, tn, m) in [(bg, cg[ci], f"cg{ci}", j), (bu, cu[ci], f"cu{ci}", 22 + j)]:
                        ACTV(T[:], bk(bb2)[:, 1:257], AF.Identity, [f"bank{bb2}", "fcw", "fcb"], [tn],
                             scale=fcw[:, m * 3 + 1:m * 3 + 2], bias=fcb[:, m:m + 1])
                        STT(T[:], bk(bb2)[:, 0:256], fcw[:, m * 3:m * 3 + 1], T[:], ALU.mult, ALU.add, [f"bank{bb2}", tn], [tn])
                        STT(T[:], bk(bb2)[:, 2:258], fcw[:, m * 3 + 2:m * 3 + 3], T[:], ALU.mult, ALU.add, [f"bank{bb2}", tn], [tn])
                    ACTV(sgt[ci][:], cg[ci][:], AF.Silu, [f"cg{ci}"], [f"sgt{ci}"])
                    TT("pool", actT[:, j, sub * 256:(sub + 1) * 256], sgt[ci][:], cu[ci][:], ALU.mult, [f"sgt{ci}", f"cu{ci}"], ["actT"])
            for ti in range(4):
                tok0 = T0 + ti * 128
                i = tcn[0] % 2
                tcn[0] += 1
                DMA(h3[i][:], H2[tok0:tok0 + 128, :], [f"H2_{tok0 // 128}"], [f"h3_{i}"])
                for hf in range(2):
                    cs = slice(hf * 512, (hf + 1) * 512)
                    b = getbank()
                    MM([(bk(b), actT[:, j, ti * 128:(ti + 1) * 128], wdn[:, j, cs], j == 0, j == 21) for j in range(22)],
                       ["actT", "wdn"], [f"bank{b}"])
                    TT("dve", h3[i][:, cs], bk(b), h3[i][:, cs], ALU.add, [f"bank{b}", f"h3_{i}"], [f"h3_{i}"])
                ACTV(junkd, h3[i][:], AF.Square, [f"h3_{i}"], ["cstage", f"ssd{i}"], accum=ssd_[i][:])
                ACTV(rsd[i][:], ssd_[i][:], AF.Sqrt, [f"ssd{i}"], [f"rsd{i}"], scale=1.0 / D, bias=EPS)
                P.op("dve", lambda e, i=i: e.reciprocal(out=rsd[i][:], in_=rsd[i][:]), [f"rsd{i}"], [f"rsd{i}"])
                STT(ot2[i][:], h3[i][:], rsd[i][:], fnw_bc[:], ALU.mult, ALU.mult, [f"h3_{i}", f"rsd{i}", "fnw_bc"], [f"ot2_{i}"])
                DMA(out[tok0 - 128:tok0, :], ot2[i][:], [f"ot2_{i}"], [f"out{tok0}"])
        sD.close()
        sC.close()
        P.emit(nc)
    return nc


def _host_inputs(inputs, b):
    f = lambda a: np.ascontiguousarray(np.asarray(a, dtype=np.float32))
    m = {
        "x": f(inputs["x"][b]),
        "meta": f(inputs["meta_tokens"]),
        "w_in": f(inputs["w_in"][0]),
        "w_ret_out": f(inputs["w_ret_out"][0]),
        "w_ssd_out": f(inputs["w_ssd_out"][0]),
        "w_out": f(inputs["w_out"][0]),
        "w_ffn_up": f(inputs["w_ffn_up"][0]),
        "w_ffn_down": f(inputs["w_ffn_down"][0]),
        "r_nmw": f(inputs["norm_mix_w"][0]), "r_nfw": f(inputs["norm_ffn_w"][0]), "r_fnw": f(inputs["final_norm_w"]),
        "r_dtb": f(np.concatenate([inputs["dt_bias_f"][0], inputs["dt_bias_b"][0]])),
        "r_alog": f(np.concatenate([inputs["a_log_f"][0], inputs["a_log_b"][0]])),
        "r_dskip": f(inputs["d_skip"][0]),
        "p_scw": f(np.asarray(inputs["w_ssd_conv"][0]).reshape(3, 24, 128).transpose(2, 1, 0).reshape(128, 72)),
        "p_scb": f(np.asarray(inputs["b_ssd_conv"][0]).reshape(24, 128).T),
        "p_fcw": f(np.asarray(inputs["w_ffn_conv"][0]).reshape(3, 44, 128).transpose(2, 1, 0).reshape(128, 132)),
        "p_fcb": f(np.asarray(inputs["b_ffn_conv"][0]).reshape(44, 128).T),
        "p_ncol": f(np.concatenate([np.asarray(inputs["ret_gn_w"][0]).reshape(8, 128).T,
                                    np.asarray(inputs["ssd_norm_w"][0]).reshape(16, 128).T], axis=1)),
    }
    return m


def kernel(**inputs):
    nc = build()
    cst = _consts()
    in_maps = []
    for b in range(8):
        m = _host_inputs(inputs, b)
        m.update(cst)
        in_maps.append(m)
    res = run_bass_kernel_spmd(nc, in_maps, core_ids=list(range(8)))
    return np.stack([np.asarray(r["out"], dtype=np.float32) for r in res.results], 0)
```
